# Optimizing a Trainium2 kernel written in Bass

```python
import math
import jax, jax.numpy as jnp
from jax import lax
import numpy as np

D_MODEL = 1024
BATCH = 16
SEQ = 4096
DEPTH = 1
DEC_BATCH = 8
DEC_SEQ = 32
PAST_LEN = 1024

CHUNK = 64
WINDOW = 128
WINDOW_CHUNKS = WINDOW // CHUNK
ATTN_WIDTH = D_MODEL // 2
SSM_WIDTH = D_MODEL - ATTN_WIDTH
HEAD_DIM = 64
N_HEADS = ATTN_WIDTH // HEAD_DIM
N_KV_HEADS = 2
GQA_REP = N_HEADS // N_KV_HEADS
SSM_CH = 16
SSM_GROUPS = SSM_WIDTH // SSM_CH
SSM_STATE = 64
D_FF = -(-8 * D_MODEL // (3 * 256)) * 256
ROPE_THETA = 10000.0
EPS = 1e-6
Q_COLS = N_HEADS * HEAD_DIM
KV_COLS = N_KV_HEADS * HEAD_DIM
IN_COLS = Q_COLS + 2 * KV_COLS + SSM_WIDTH

kernel_name = "hybrid_streaming_s5_swa_step"


def _rmsnorm(x, g):
    xf = x.astype(jnp.float32)
    y = xf * lax.rsqrt(jnp.mean(xf * xf, axis=-1, keepdims=True) + EPS)
    return (y * g.astype(jnp.float32)).astype(x.dtype)


def _rope(x, pos):
    half = HEAD_DIM // 2
    inv = ROPE_THETA ** (-jnp.arange(half, dtype=jnp.float32) * 2.0 / HEAD_DIM)
    ang = pos.astype(jnp.float32)[:, None] * inv[None, :]
    cos = jnp.cos(ang)[None, :, None, :]
    sin = jnp.sin(ang)[None, :, None, :]
    xf = x.astype(jnp.float32)
    x1, x2 = xf[..., :half], xf[..., half:]
    return jnp.concatenate([x1 * cos - x2 * sin, x2 * cos + x1 * sin], axis=-1).astype(x.dtype)


def _sink_attention(q, k, v, valid, sinks):
    s = jnp.einsum('bnqgrd,bnkgd->bngrqk', q.astype(jnp.float32), k.astype(jnp.float32)) * (HEAD_DIM ** -0.5)
    s = jnp.where(valid[None, :, None, None, None, :], s, -jnp.inf)
    sink = sinks.astype(jnp.float32).reshape(N_KV_HEADS, GQA_REP)[None, None, :, :, None]
    m = jnp.maximum(jnp.max(s, axis=-1), sink)
    p = jnp.exp(s - m[..., None])
    p = p / (jnp.sum(p, axis=-1) + jnp.exp(sink - m))[..., None]
    o = jnp.einsum('bngrqk,bnkgd->bnqgrd', p, v.astype(jnp.float32))
    return o.astype(q.dtype)


def _swa_prompt(q, k, v, sinks):
    bsz, s = q.shape[0], q.shape[1]
    nc = s // CHUNK
    qb = q.reshape(bsz, nc, CHUNK, N_KV_HEADS, GQA_REP, HEAD_DIM)

    def band(t):
        tp = jnp.pad(t, ((0, 0), (WINDOW, 0), (0, 0), (0, 0)))
        tp = tp.reshape(bsz, nc + WINDOW_CHUNKS, CHUNK, N_KV_HEADS, HEAD_DIM)
        return jnp.concatenate([tp[:, j:j + nc] for j in range(WINDOW_CHUNKS + 1)], axis=2)

    key_chunk = (jnp.arange(nc)[:, None]
                 + jnp.repeat(jnp.arange(WINDOW_CHUNKS + 1), CHUNK)[None, :] - WINDOW_CHUNKS)
    o = _sink_attention(qb, band(k), band(v), key_chunk >= 0, sinks)
    return o.reshape(bsz, s, ATTN_WIDTH)


def _swa_sample(q, k, v, past_k, past_v, sinks):
    bsz, s = q.shape[0], q.shape[1]
    qb = q.reshape(bsz, 1, s, N_KV_HEADS, GQA_REP, HEAD_DIM)
    kb = jnp.concatenate([past_k.astype(k.dtype), k], axis=1)[:, None]
    vb = jnp.concatenate([past_v.astype(v.dtype), v], axis=1)[:, None]
    valid = jnp.ones((1, kb.shape[2]), dtype=bool)
    o = _sink_attention(qb, kb, vb, valid, sinks)
    return o.reshape(bsz, s, ATTN_WIDTH)


def _ssm_combine(left, right):
    a1, b1 = left
    a2, b2 = right
    return a1 * a2, a2 * b1 + b2


def _s5(u, h0, a_re, a_im, log_dt, b_re, b_im, c_re, c_im, d_skip):
    bsz, s = u.shape[0], u.shape[1]
    f32 = jnp.float32
    uf = u.astype(f32).reshape(bsz, s, SSM_GROUPS, SSM_CH)
    lam = lax.complex(a_re.astype(f32), a_im.astype(f32))
    dt = jnp.exp(log_dt.astype(f32))[:, None]
    lam_bar = jnp.exp(lam * dt)
    b_bar = ((lam_bar - 1.0) / lam)[:, :, None] * lax.complex(b_re.astype(f32), b_im.astype(f32))
    bu = jnp.einsum('gph,bsgh->bsgp', b_bar, uf.astype(jnp.complex64))
    a = jnp.broadcast_to(lam_bar[None, None], (1, s, SSM_GROUPS, SSM_STATE))
    a_cum, xs = lax.associative_scan(_ssm_combine, (a, bu), axis=1)
    if h0 is not None:
        xs = xs + a_cum * h0[:, None]
    c = lax.complex(c_re.astype(f32), c_im.astype(f32))
    y = jnp.einsum('ghp,bsgp->bsgh', c, xs).real + d_skip.astype(f32).reshape(SSM_GROUPS, SSM_CH) * uf
    return y.reshape(bsz, s, SSM_WIDTH), xs[:, -1]


def _layer(x, c, pos, past_k, past_v, h0, lw):
    bsz, s = x.shape[0], x.shape[1]
    mod = jax.nn.silu(c) @ lw['w_ada'] + lw['b_ada']
    sh1, sc1, g1, sh2, sc2, g2 = jnp.split(mod[:, None, :], 6, axis=-1)

    h = _rmsnorm(x, lw['ln1_g']) * (1.0 + sc1) + sh1
    proj = h @ lw['w_in']
    q, k, v, u = jnp.split(proj, [Q_COLS, Q_COLS + KV_COLS, Q_COLS + 2 * KV_COLS], axis=-1)
    q = _rope(_rmsnorm(q.reshape(bsz, s, N_HEADS, HEAD_DIM), lw['q_norm_g']), pos)
    k = _rope(_rmsnorm(k.reshape(bsz, s, N_KV_HEADS, HEAD_DIM), lw['k_norm_g']), pos)
    v = v.reshape(bsz, s, N_KV_HEADS, HEAD_DIM)
    if past_k is None:
        attn = _swa_prompt(q, k, v, lw['attn_sinks'])
        k_rows, v_rows = k[:, -WINDOW:], v[:, -WINDOW:]
    else:
        attn = _swa_sample(q, k, v, past_k, past_v, lw['attn_sinks'])
        k_rows, v_rows = k, v

    ssm_y, h_last = _s5(u, h0, lw['ssm_A_re'], lw['ssm_A_im'], lw['ssm_log_dt'], lw['ssm_B_re'],
                        lw['ssm_B_im'], lw['ssm_C_re'], lw['ssm_C_im'], lw['ssm_D'])
    g = jax.nn.gelu(ssm_y)
    ssm_o = (g * jax.nn.sigmoid(g @ lw['ssm_glu_w'].astype(jnp.float32) + lw['ssm_glu_b'].astype(jnp.float32))).astype(x.dtype)

    merged = jnp.concatenate([_rmsnorm(attn, lw['attn_out_g']), _rmsnorm(ssm_o, lw['ssm_out_g'])], axis=-1)
    x = x + g1 * (merged @ lw['w_out'])

    h2 = _rmsnorm(x, lw['ln2_g']) * (1.0 + sc2) + sh2
    ff = (jax.nn.silu(h2 @ lw['w_gate']) * (h2 @ lw['w_up'])) @ lw['w_down']
    x = x + g2 * ff
    state = jnp.stack([h_last.real, h_last.imag], axis=-1)
    return x, k_rows, v_rows, state


def setup_inputs(seed: int = 0) -> dict:
    key = jax.random.key(seed)
    ks = iter(jax.random.split(key, 40))
    f32 = jnp.float32
    nrm = lambda shape, scale: jax.random.normal(next(ks), shape, f32) * scale
    gain = lambda shape: 1.0 + nrm(shape, 0.02)
    n_idx = jnp.arange(SSM_STATE, dtype=f32)
    return {
        "x_prompt": nrm((BATCH, SEQ, D_MODEL), 1.0),
        "x_sample": nrm((DEC_BATCH, DEC_SEQ, D_MODEL), 1.0),
        "cache_k": nrm((DEPTH, DEC_BATCH, WINDOW, N_KV_HEADS, HEAD_DIM), 1.0),
        "cache_v": nrm((DEPTH, DEC_BATCH, WINDOW, N_KV_HEADS, HEAD_DIM), 1.0),
        "state_ssm": nrm((DEPTH, DEC_BATCH, SSM_GROUPS, SSM_STATE, 2), 0.5),
        "c_prompt": nrm((BATCH, D_MODEL), 1.0),
        "c_sample": nrm((DEC_BATCH, D_MODEL), 1.0),
        "w_ada": nrm((DEPTH, D_MODEL, 6 * D_MODEL), 0.5 * D_MODEL ** -0.5),
        "b_ada": nrm((DEPTH, 6 * D_MODEL), 0.02),
        "ln1_g": gain((DEPTH, D_MODEL)),
        "w_in": nrm((DEPTH, D_MODEL, IN_COLS), D_MODEL ** -0.5),
        "q_norm_g": gain((DEPTH, HEAD_DIM)),
        "k_norm_g": gain((DEPTH, HEAD_DIM)),
        "attn_sinks": nrm((DEPTH, N_HEADS), 0.5),
        "ssm_A_re": -0.5 + nrm((DEPTH, SSM_GROUPS, SSM_STATE), 0.01),
        "ssm_A_im": jnp.pi * n_idx + nrm((DEPTH, SSM_GROUPS, SSM_STATE), 0.01),
        "ssm_log_dt": jax.random.uniform(next(ks), (DEPTH, SSM_GROUPS), f32, math.log(1e-3), math.log(1e-1)),
        "ssm_B_re": nrm((DEPTH, SSM_GROUPS, SSM_STATE, SSM_CH), (2 * SSM_CH) ** -0.5),
        "ssm_B_im": nrm((DEPTH, SSM_GROUPS, SSM_STATE, SSM_CH), (2 * SSM_CH) ** -0.5),
        "ssm_C_re": nrm((DEPTH, SSM_GROUPS, SSM_CH, SSM_STATE), SSM_STATE ** -0.5),
        "ssm_C_im": nrm((DEPTH, SSM_GROUPS, SSM_CH, SSM_STATE), SSM_STATE ** -0.5),
        "ssm_D": nrm((DEPTH, SSM_WIDTH), 1.0),
        "ssm_glu_w": nrm((DEPTH, SSM_WIDTH, SSM_WIDTH), SSM_WIDTH ** -0.5),
        "ssm_glu_b": nrm((DEPTH, SSM_WIDTH), 0.02),
        "attn_out_g": gain((DEPTH, ATTN_WIDTH)),
        "ssm_out_g": gain((DEPTH, SSM_WIDTH)),
        "w_out": nrm((DEPTH, D_MODEL, D_MODEL), D_MODEL ** -0.5),
        "ln2_g": gain((DEPTH, D_MODEL)),
        "w_gate": nrm((DEPTH, D_MODEL, D_FF), D_MODEL ** -0.5),
        "w_up": nrm((DEPTH, D_MODEL, D_FF), D_MODEL ** -0.5),
        "w_down": nrm((DEPTH, D_FF, D_MODEL), D_FF ** -0.5),
    }


def reference(x_prompt, x_sample, cache_k, cache_v, state_ssm, c_prompt, c_sample,
              w_ada, b_ada, ln1_g, w_in, q_norm_g, k_norm_g, attn_sinks,
              ssm_A_re, ssm_A_im, ssm_log_dt, ssm_B_re, ssm_B_im, ssm_C_re, ssm_C_im,
              ssm_D, ssm_glu_w, ssm_glu_b, attn_out_g, ssm_out_g, w_out,
              ln2_g, w_gate, w_up, w_down):
    y_prompt, y_sample = x_prompt, x_sample
    pos_prompt = jnp.arange(x_prompt.shape[1])
    pos_sample = PAST_LEN + jnp.arange(x_sample.shape[1])
    kp_l, vp_l, hp_l, ks_l, vs_l, hs_l = [], [], [], [], [], []
    for l in range(DEPTH):
        lw = dict(w_ada=w_ada[l], b_ada=b_ada[l], ln1_g=ln1_g[l], w_in=w_in[l],
                  q_norm_g=q_norm_g[l], k_norm_g=k_norm_g[l], attn_sinks=attn_sinks[l],
                  ssm_A_re=ssm_A_re[l], ssm_A_im=ssm_A_im[l], ssm_log_dt=ssm_log_dt[l],
                  ssm_B_re=ssm_B_re[l], ssm_B_im=ssm_B_im[l], ssm_C_re=ssm_C_re[l],
                  ssm_C_im=ssm_C_im[l], ssm_D=ssm_D[l], ssm_glu_w=ssm_glu_w[l],
                  ssm_glu_b=ssm_glu_b[l], attn_out_g=attn_out_g[l], ssm_out_g=ssm_out_g[l],
                  w_out=w_out[l], ln2_g=ln2_g[l], w_gate=w_gate[l], w_up=w_up[l],
                  w_down=w_down[l])
        y_prompt, kp, vp, hp = _layer(y_prompt, c_prompt, pos_prompt, None, None, None, lw)
        h0 = lax.complex(state_ssm[l, ..., 0].astype(jnp.float32), state_ssm[l, ..., 1].astype(jnp.float32))
        y_sample, ksn, vsn, hsn = _layer(y_sample, c_sample, pos_sample, cache_k[l], cache_v[l], h0, lw)
        kp_l.append(kp); vp_l.append(vp); hp_l.append(hp)
        ks_l.append(ksn); vs_l.append(vsn); hs_l.append(hsn)
    return (y_prompt, y_sample, jnp.stack(kp_l), jnp.stack(vp_l), jnp.stack(hp_l),
            jnp.stack(ks_l), jnp.stack(vs_l), jnp.stack(hs_l))
```

```python
import math
import contextlib
import numpy as np
import concourse.bass as bass
import concourse.mybir as mybir
from concourse.bass_utils import run_bass_kernel_spmd

F32 = mybir.dt.float32
BF16 = mybir.dt.bfloat16
AF = mybir.ActivationFunctionType
ALU = mybir.AluOpType
AX = mybir.AxisListType

DM = 1024
NH = 8
HD = 64
NKV = 2
SSMW = 512
NG = 32
NP = 64
NCH = 16
DFF = 2816
INC = 1280
EPS = 1e-6
PAST = 1024
D = 4
NFT = DFF // 128


class Prog:
    NS = 8
    PARANOID = ("act", "dve", "pool")
    LOOKAHEAD = 48

    def __init__(self, nc, es):
        import os
        if os.environ.get("KPAR") is not None:
            self.PARANOID = tuple(x for x in os.environ["KPAR"].split(",") if x)
        self.nc = nc
        self.eng = {"pe": nc.tensor, "act": nc.scalar, "dve": nc.vector, "pool": nc.gpsimd, "sp": nc.sync}
        self.ops = []
        self.sem = {e: es.enter_context(nc.semaphore("s_" + e)) for e in ("pe", "act", "dve", "pool")}
        self.dsem = {q: [es.enter_context(nc.semaphore("d_%s%d" % (q, i))) for i in range(self.NS)]
                     for q in ("sp", "pool", "act")}

    mute = False

    def stage(self, name):
        import os
        if os.environ.get("KSTAGE") == name:
            self.mute = True

    def op(self, eng, fn, r=(), w=(), dma=False, n=None):
        if self.mute:
            return
        self.ops.append(dict(eng=eng, fn=fn, r=tuple(r), w=tuple(w), dma=dma, deps=set(), sig=False, n=n))

    def dma(self, q, fn, r=(), w=(), n=None):
        self.op(q, fn, r, w, dma=True, n=n)

    COST = {"pe": (0.03, 1 / 2400.0, 128), "act": (0.22, 1 / 1200.0, 256), "dve": (0.1, 1 / 960.0, 256),
            "pool": (0.2, 1 / 450.0, 256), "sp": (2.0, 0.0, 0)}

    @staticmethod
    def _auto_n(o):
        e = o["eng"]
        r = " ".join(o["r"]); w = " ".join(o["w"])
        has = lambda t, k: k in t
        if e == "pe":
            if has(r, "Wtab") or has(r, "Ktab") or has(r, "Vtab"):
                return 64
            if has(r, "Wout") or has(r, "Wg_") or has(r, "Wu_") or has(r, "Wd_") or has(r, "mrow"):
                return 512
            if has(r, "kT") and has(r, "qT"):
                return 512
            if has(r, "Win"):
                return 384
            if has(r, "Wglu") or has(r, "ones_b"):
                return 256
            if has(r, "PT"):
                return 65
            return 128
        if e == "act":
            for k, n in (("junk", 1024), ("xn", 1024), ("Hbf", 1024), ("sq", 512), ("PT", 512), ("qT", 512), ("mTb", 512), ("sg", 512),
                         ("uT", 256), ("gT", 256), ("sig", 256), ("rsb", 256), ("Vaug", 128)):
                if has(w, k):
                    return n
            return 64
        if e == "dve":
            if has(w, "rsb"):
                return 2048
            for k, n in (("tmp", 1024), ("attnb", 512), ("qn", 512), ("aT", 512), ("sA", 512), ("sB", 512), ("sC", 512), ("sD", 512),
                         ("hT", 256), ("qr", 256), ("attn", 256), ("soT", 256), ("mTb", 256), ("kr", 64)):
                if has(w, k):
                    return n
            return 64
        if e == "pool":
            for k, n in (("Xr", 1024), ("sA", 1024), ("sC", 1024), ("sD", 1024), ("qn", 640), ("ra", 320), ("rb", 320), ("sqT", 256)):
                if has(w, k):
                    return n
            return 128
        return 0

    def _cost(self, o):
        a, b, dn = self.COST.get(o["eng"], (0.3, 0.0, 0))
        if o["dma"]:
            return 2.0 if o["n"] is None else o["n"]
        n = o["n"] if o["n"] is not None else self._auto_n(o)
        return a + b * n

    SCHED_ENG = ("pe", "act", "dve", "pool", "sp")

    def schedule(self, K=6):
        import heapq, os
        if os.environ.get("KSCHEDE") is not None:
            self.SCHED_ENG = tuple(os.environ["KSCHEDE"].split(","))
        out = []
        seg = []
        for o in self.ops + [dict(barrier=True)]:
            if o.get("barrier"):
                out.extend(self._sched_seg(seg, K))
                out.append(o)
                seg = []
            else:
                seg.append(o)
        out.pop()
        self.ops = out

    def _sched_seg(self, ops, K):
        import heapq
        n = len(ops)
        if n == 0:
            return []
        lastw = {}
        readers = {}
        preds = [set() for _ in range(n)]
        for i, o in enumerate(ops):
            for k in o["r"]:
                if k in lastw:
                    preds[i].add(lastw[k])
            for k in o["w"]:
                if k in lastw:
                    preds[i].add(lastw[k])
                preds[i].update(readers.get(k, ()))
            for k in o["r"]:
                readers.setdefault(k, []).append(i)
            for k in o["w"]:
                lastw[k] = i
                readers[k] = []
            preds[i].discard(i)
        succ = [[] for _ in range(n)]
        indeg = [len(p) for p in preds]
        for i, p in enumerate(preds):
            for j in p:
                succ[j].append(i)
        avail = {}
        ready_t = [0.0] * n
        for i in range(n):
            if indeg[i] == 0:
                heapq.heappush(avail.setdefault(ops[i]["eng"], []), i)
        eng_free = {}
        start = [0.0] * n
        done = 0
        while done < n:
            best = None
            for e, hp in avail.items():
                if not hp:
                    continue
                cand = heapq.nsmallest(K if e in self.SCHED_ENG else 1, hp)
                ef = eng_free.get(e, 0.0)
                ci = min(cand, key=lambda i: (max(ef, ready_t[i]), i))
                st = max(ef, ready_t[ci])
                if best is None or (st, ci) < (best[0], best[1]):
                    best = (st, ci, e)
            st, i, e = best
            avail[e].remove(i)
            heapq.heapify(avail[e])
            o = ops[i]
            c = self._cost(o)
            start[i] = st
            if o["dma"]:
                eng_free[e] = st + 0.06
            else:
                eng_free[e] = st + c
            fin = st + c
            for j in succ[i]:
                indeg[j] -= 1
                if fin > ready_t[j]:
                    ready_t[j] = fin
                if indeg[j] == 0:
                    heapq.heappush(avail.setdefault(ops[j]["eng"], []), j)
            done += 1
        order = sorted(range(n), key=lambda i: (start[i], i))
        self.sim_time = getattr(self, "sim_time", 0.0) + max(start[i] + self._cost(ops[i]) for i in range(n))
        return [ops[i] for i in order]

    def barrier(self):
        self.ops.append(dict(barrier=True))

    def emit(self):
        raw = self.ops
        ops = []
        last_eng = {}
        recent_dma = {}
        bar_deps = None
        need = {}
        for o in raw:
            if o.get("barrier"):
                bar_deps = set(last_eng.values())
                for q, lst in recent_dma.items():
                    bar_deps.update(lst[-self.NS:])
                need = {e: True for e in self.eng}
                continue
            i = len(ops)
            ops.append(o)
            if need.get(o["eng"]):
                o["deps"].update(bar_deps)
                need[o["eng"]] = False
            last_eng[o["eng"]] = i
            if o["dma"]:
                recent_dma.setdefault(o["eng"], []).append(i)
        self.ops = ops
        lastw = {}
        readers = {}
        for i, o in enumerate(ops):
            for k in o["r"]:
                if k in lastw:
                    o["deps"].add(lastw[k])
            for k in o["w"]:
                if k in lastw:
                    o["deps"].add(lastw[k])
                for j in readers.get(k, ()):
                    o["deps"].add(j)
            for k in o["r"]:
                lst = readers.setdefault(k, [])
                if not o["dma"]:
                    lst[:] = [j for j in lst if ops[j]["dma"] or ops[j]["eng"] != o["eng"]]
                lst.append(i)
            for k in o["w"]:
                lastw[k] = i
                readers[k] = []
        for i, o in enumerate(ops):
            o["deps"].discard(i)
            for j in o["deps"]:
                p = ops[j]
                if p["dma"] or o["dma"] or p["eng"] != o["eng"] or p["eng"] in self.PARANOID:
                    p["sig"] = True
        cnt = {e: 0 for e in self.sem}
        dcnt = {q: 0 for q in self.dsem}
        waited = {}
        tot_wait = 0
        eng_ops = {}
        for i, o in enumerate(ops):
            eng_ops.setdefault(o["eng"], []).append(i)
        eng_pos = {e: 0 for e in eng_ops}
        for i, o in enumerate(ops):
            e = o["eng"]
            E = self.eng[e]
            pos = eng_pos[e]
            eng_pos[e] = pos + 1
            for j in sorted(o["deps"]):
                p = ops[j]
                if not p["sig"]:
                    continue
                if p["eng"] == e and not p["dma"] and not o["dma"] and e not in self.PARANOID:
                    continue
                s, v = p["sigval"]
                key = (e, id(s))
                if waited.get(key, 0) >= v:
                    continue
                for i2 in eng_ops[e][pos + 1:pos + 1 + self.LOOKAHEAD]:
                    for j2 in ops[i2]["deps"]:
                        if j2 < i and ops[j2]["sig"] and not ops[j2]["dma"] and "sigval" in ops[j2]:
                            s2, v2 = ops[j2]["sigval"]
                            if s2 is s and v2 > v:
                                v = v2
                E.wait_ge(s, v)
                tot_wait += 1
                waited[key] = v
            if o["dma"]:
                n = dcnt[e]
                s = self.dsem[e][n % self.NS]
                prev = 16 * (n // self.NS)
                if prev > 0:
                    key = (e, id(s))
                    if waited.get(key, 0) < prev:
                        E.wait_ge(s, prev)
                        waited[key] = prev
                ins = o["fn"]()
                ins.then_inc(s, 16)
                o["sigval"] = (s, prev + 16)
                o["sig"] = True
                dcnt[e] = n + 1
            else:
                ins = o["fn"]()
                if o["sig"]:
                    cnt[e] += 1
                    ins.then_inc(self.sem[e], 1)
                    o["sigval"] = (self.sem[e], cnt[e])
        sp = self.eng["sp"]
        for q in self.dsem:
            n = dcnt[q]
            for t in range(max(0, n - self.NS), n):
                sp.wait_ge(self.dsem[q][t % self.NS], 16 * (t // self.NS + 1))
        self.stats = dict(n_ops=len(ops), n_wait=tot_wait, sim_us=getattr(self, "sim_time", None))


def build_program(S, NSEQ=2, TS=32):
    nc = bass.Bass("TRN2", target_bir_lowering=False)
    dt_in = lambda name, shape: nc.dram_tensor(name, list(shape), F32, kind="ExternalInput").ap()
    dt_out = lambda name, shape: nc.dram_tensor(name, list(shape), F32, kind="ExternalOutput").ap()

    xp = dt_in("xp", [NSEQ, S, DM])
    xs = dt_in("xs", [TS, DM])
    ck = dt_in("ck", [128, 128])
    cv = dt_in("cv", [128, 128])
    st0 = dt_in("st0", [NG, NP, 2])
    c3 = dt_in("c3", [NSEQ + 1, DM])
    w_ada = dt_in("w_ada", [DM, 6 * DM])
    b_ada = dt_in("b_ada", [6 * DM])
    ln1_g = dt_in("ln1_g", [DM])
    w_in = dt_in("w_in", [DM, INC])
    q_norm_g = dt_in("q_norm_g", [HD])
    k_norm_g = dt_in("k_norm_g", [HD])
    attn_sinks = dt_in("attn_sinks", [NH])
    A_re = dt_in("ssm_A_re", [NG, NP])
    A_im = dt_in("ssm_A_im", [NG, NP])
    log_dt = dt_in("ssm_log_dt", [NG])
    B_re = dt_in("ssm_B_re", [NG, NP, NCH])
    B_im = dt_in("ssm_B_im", [NG, NP, NCH])
    C_re = dt_in("ssm_C_re", [NG, NCH, NP])
    C_im = dt_in("ssm_C_im", [NG, NCH, NP])
    ssm_D = dt_in("ssm_D", [SSMW])
    glu_w = dt_in("ssm_glu_w", [SSMW, SSMW])
    glu_b = dt_in("ssm_glu_b", [SSMW])
    attn_out_g = dt_in("attn_out_g", [512])
    ssm_out_g = dt_in("ssm_out_g", [SSMW])
    w_out = dt_in("w_out", [DM, DM])
    ln2_g = dt_in("ln2_g", [DM])
    w_gate = dt_in("w_gate", [DM, DFF])
    w_up = dt_in("w_up", [DM, DFF])
    w_down = dt_in("w_down", [DFF, DM])
    ident_in = dt_in("ident", [128, 128])
    bmask_in = dt_in("bmask", [128, 128])
    cosp_in = dt_in("cosp", [S, 32])
    sinp_in = dt_in("sinp", [S, 32])
    coss_in = dt_in("coss", [TS, 32])
    sins_in = dt_in("sins", [TS, 32])
    mrow_in = dt_in("mrow", [1, 256])
    mcol_in = dt_in("mcol", [1, 256])
    rowmask_in = dt_in("rowmask", [128, 2])

    yp = dt_out("yp", [NSEQ, S, DM])
    ys = dt_out("ys", [TS, DM])
    kp = dt_out("kp", [NSEQ, 128, 128])
    vp = dt_out("vp", [NSEQ, 128, 128])
    hp = dt_out("hp", [NSEQ, NG, NP, 2])
    ks = dt_out("ks", [TS, 128])
    vs = dt_out("vs", [TS, 128])
    hs = dt_out("hs", [NG, NP, 2])

    NTOK = NSEQ * S + TS
    x1d = nc.dram_tensor("x1d", [NTOK, DM], F32, kind="Internal").ap()
    gbd = nc.dram_tensor("gbd", [2 * (NSEQ + 1), DM], F32, kind="Internal").ap()

    es = contextlib.ExitStack()
    with es:
        P = Prog(nc, es)
        SB = lambda name, shape, dt=F32: es.enter_context(nc.sbuf_tensor("sb_" + name, list(shape), dt))
        ps = es.enter_context(nc.psum_tensor("ps", [128, 8, 512], F32))
        psb = [ps[:, b, :].bitcast(BF16) for b in range(8)]
        pctr = [0]

        def bank(n=1):
            b = pctr[0]
            if n > 1:
                b = (b + n - 1) // n * n
            if b + n > 8:
                b = 0
            pctr[0] = (b + n) % 8
            return b

        def pk(b, n=1):
            return ["ps%d" % (b + i) for i in range(n)]

        p1_es = contextlib.ExitStack()
        SB1 = lambda name, shape, dt=F32: p1_es.enter_context(nc.sbuf_tensor("sb_" + name, list(shape), dt))
        NQ = NSEQ + 1
        ident_b = SB("ident_b", [128, 128], BF16)
        epsc = SB("epsc", [128, 1])
        G1T = SB("G1T", [128, 8, NQ])
        sh1T = SB("sh1T", [128, 8, NQ])
        G2T = SB("G2T", [128, 8, NQ])
        sh2T = SB("sh2T", [128, 8, NQ])
        Gb = SB("Gb", [128, DM])
        X = SB("X", [128, 4, DM])
        xn = SB("xn", [128, 2, DM], BF16)
        junk = SB("junk", [128, DM], BF16)
        hT = SB("hT", [128, 8, 512], BF16)
        ss4 = SB("ss4", [128, 4]); rs4 = SB("rs4", [128, 4])
        tmp = SB("tmp", [128, DM])
        Xr2 = [SB("Xr%d" % i, [128, DM]) for i in range(2)]
        ones_b = SB1("ones_b", [128, 128], BF16)
        mrow = SB1("mrow", [1, 256], BF16)
        mcol = SB1("mcol", [1, 256], BF16)
        Win = SB1("Win", [128, 8, INC], BF16)
        Wout = SB1("Wout", [128, 8, DM], BF16)
        Wglu = SB1("Wglu", [128, 4, SSMW], BF16)
        glubT = SB1("glubT", [128, 4])
        gsoT = SB1("gsoT", [128, 4])
        gao = SB1("gao", [128, 512], BF16)
        g64 = SB1("g64", [128, 128])
        gqk = SB1("gqk", [128, 640])
        esink = SB1("esink", [128, 8])
        cosp = SB1("cosp", [128, S // 128, 32])
        sinp = SB1("sinp", [128, S // 128, 32])
        coss = SB1("coss", [128, 32])
        sins = SB1("sins", [128, 32])
        Wtab = SB1("Wtab", [128, 4, D, 2, 2, 128], BF16)
        Vtab = SB1("Vtab", [128, 16, D, 2, 64], BF16)
        Ktab = SB1("Ktab", [128, 4, D, 128], BF16)
        Ere = SB1("Ere", [128, 16, 64])
        Eim = SB1("Eim", [128, 16, 64])
        rDt = SB1("rDt", [128, 16])
        E1re = SB1("E1re", [128, 16])
        E1im = SB1("E1im", [128, 16])
        Hre = SB1("Hre", [128, 16])
        Him = SB1("Him", [128, 16])

        P.op("pool", lambda: nc.gpsimd.memset(epsc[:], EPS), w=["epsc"])
        P.dma("pool", lambda: nc.gpsimd.dma_start(out=ident_b[:], in_=ident_in[:, :]), w=["ident_b"])
        P.dma("pool", lambda: nc.gpsimd.dma_start(out=mrow[:], in_=mrow_in[:, :]), w=["mrow"])
        P.dma("pool", lambda: nc.gpsimd.dma_start(out=mcol[:], in_=mcol_in[:, :]), w=["mcol"])
        P.op("pool", lambda: nc.gpsimd.memset(ones_b[:], 1.0), w=["ones_b"])
        P.dma("sp", lambda: nc.sync.dma_start(out=cosp[:], in_=cosp_in.rearrange("(t p) f -> p t f", p=128)), w=["rope"])
        P.dma("sp", lambda: nc.sync.dma_start(out=sinp[:], in_=sinp_in.rearrange("(t p) f -> p t f", p=128)), w=["rope"])
        P.dma("sp", lambda: nc.sync.dma_start(out=coss[0:TS, :], in_=coss_in[:, :]), w=["rope"])
        P.dma("sp", lambda: nc.sync.dma_start(out=sins[0:TS, :], in_=sins_in[:, :]), w=["rope"])
        for k in range(8):
            for g in range(2):
                P.dma("pool", lambda k=k, g=g: nc.gpsimd.dma_start(
                    out=Win[:, k, 0:512].rearrange("p (j g d) -> p g j d", j=4, g=2)[:, g],
                    in_=w_in[k * 128:(k + 1) * 128, 256 * g:256 * g + 256].rearrange("p (j d) -> p j d", j=4)), w=["Win"])
            P.dma("pool", lambda k=k: nc.gpsimd.dma_start(out=Win[:, k, 512:INC], in_=w_in[k * 128:(k + 1) * 128, 512:INC]), w=["Win"])
            P.dma("pool", lambda k=k: nc.gpsimd.dma_start(out=Wout[:, k, :], in_=w_out[k * 128:(k + 1) * 128, :]), w=["Wout%d" % k])
        P.dma("pool", lambda: nc.gpsimd.dma_start(out=Wglu[:], in_=glu_w.rearrange("(k p) n -> p k n", p=128)), w=["Wglu"])
        P.dma("sp", lambda: nc.sync.dma_start(out=glubT[:], in_=glu_b.rearrange("(k p) -> p k", p=128)), w=["glubT"])
        P.dma("sp", lambda: nc.sync.dma_start(out=gsoT[:], in_=ssm_out_g.rearrange("(k p) -> p k", p=128)), w=["gsoT"])
        P.dma("pool", lambda: nc.gpsimd.dma_start(out=gao[:], in_=attn_out_g.partition_broadcast(128)), w=["gao"])
        P.dma("sp", lambda: nc.sync.dma_start(out=g64[:, 0:64], in_=q_norm_g.partition_broadcast(128)), w=["g64"])
        P.dma("sp", lambda: nc.sync.dma_start(out=g64[:, 64:128], in_=k_norm_g.partition_broadcast(128)), w=["g64"])
        P.op("dve", lambda: nc.vector.tensor_scalar(out=gqk[:, 0:512].rearrange("p (h d) -> p h d", h=8),
                                                    in0=g64[:, 0:64].unsqueeze(1).to_broadcast([128, 8, HD]),
                                                    scalar1=HD ** -0.5, scalar2=None, op0=ALU.mult), r=["g64"], w=["gqk"])
        P.op("dve", lambda: nc.vector.tensor_copy(out=gqk[:, 512:640].rearrange("p (h d) -> p h d", h=2),
                                                  in_=g64[:, 64:128].unsqueeze(1).to_broadcast([128, 2, HD])), r=["g64"], w=["gqk"])
        P.dma("sp", lambda: nc.sync.dma_start(out=esink[:], in_=attn_sinks.partition_broadcast(128)), w=["esink"])
        P.op("act", lambda: nc.scalar.activation(out=esink[:], in_=esink[:], func=AF.Exp), r=["esink"], w=["esink"])

        P.stage("const")
        set_es = contextlib.ExitStack()
        SBs = lambda name, shape, dt=F32: set_es.enter_context(nc.sbuf_tensor("sb_" + name, list(shape), dt))
        cT = SBs("cT", [128, 8, NQ])
        silT = SBs("silT", [128, 8, 4])
        silrep = SBs("silrep", [128, NQ, 8, 128])
        wch = [SBs("wch%d" % i, [128, 8, 256]) for i in range(4)]
        bT = SBs("bT", [128, 48])
        brow = SBs("brow", [128, 256])
        lnT = SBs("lnT", [128, 16])
        modT = SBs("modT", [128, 4, 8, NQ])
        gbs = SBs("gbs", [128, NQ, 256])
        for s_ in range(NQ):
            P.dma("sp", lambda s_=s_: nc.sync.dma_start(out=cT[:, :, s_], in_=c3[s_].rearrange("(k p) -> p k", p=128)), w=["cT"])
        P.dma("sp", lambda: nc.sync.dma_start(out=bT[:], in_=b_ada.rearrange("(n p) -> p n", p=128)), w=["bT"])
        P.dma("sp", lambda: nc.sync.dma_start(out=lnT[:, 0:8], in_=ln1_g.rearrange("(k p) -> p k", p=128)), w=["lnT"])
        P.dma("sp", lambda: nc.sync.dma_start(out=lnT[:, 8:16], in_=ln2_g.rearrange("(k p) -> p k", p=128)), w=["lnT"])
        P.op("dve", lambda: nc.vector.memset(silT[:], 0.0), w=["silT"])
        P.op("act", lambda: nc.scalar.activation(out=silT[:, :, 0:NQ], in_=cT[:], func=AF.Silu), r=["cT", "silT"], w=["silT"])
        for s in range(NQ):
            P.op("dve", lambda s=s: nc.vector.tensor_copy(out=silrep[:, s, :, :], in_=silT[:, :, s:s + 1].to_broadcast([128, 8, 128])),
                 r=["silT"], w=["silrep"])
        role_of = {0: 0, 1: 1, 3: 2, 4: 3}
        for ci in range(24):
            wb_ = wch[ci % 4]
            wk = "wch%d" % (ci % 4)
            blk6 = ci // 4
            P.dma("sp", lambda ci=ci, wb_=wb_: nc.sync.dma_start(
                out=wb_[:], in_=w_ada[:, ci * 256:(ci + 1) * 256].rearrange("(k p) n -> p k n", p=128)), w=[wk])
            if blk6 in role_of:
                role = role_of[blk6]
                b = bank()
                for n2 in range(2):
                    for k in range(8):
                        P.op("pe", lambda n2=n2, k=k, b=b, wb_=wb_: nc.tensor.matmul(
                            ps[:, b, n2 * 4:n2 * 4 + 4], lhsT=wb_[:, k, n2 * 128:(n2 + 1) * 128], rhs=silT[:, k, :],
                            start=(k == 0), stop=(k == 7)), r=[wk, "silT"], w=pk(b))
                for n2 in range(2):
                    nt_ = (ci % 4) * 2 + n2
                    P.op("dve", lambda n2=n2, nt_=nt_, b=b, role=role, ci=ci: nc.vector.tensor_scalar(
                        out=modT[:, role, nt_, :], in0=ps[:, b, n2 * 4:n2 * 4 + NQ], scalar1=bT[:, ci * 2 + n2:ci * 2 + n2 + 1],
                        scalar2=None, op0=ALU.add), r=pk(b) + ["bT"], w=["modT"])
            else:
                gi = 0 if blk6 == 2 else 1
                P.dma("sp", lambda ci=ci: nc.sync.dma_start(out=brow[:], in_=b_ada[ci * 256:(ci + 1) * 256].partition_broadcast(128)),
                      w=["brow"])
                for s in range(NQ):
                    b = bank()
                    for k in range(8):
                        P.op("pe", lambda s=s, k=k, b=b, wb_=wb_: nc.tensor.matmul(
                            ps[:, b, 0:256], lhsT=silrep[:, s, k, :], rhs=wb_[:, k, :],
                            start=(k == 0), stop=(k == 7)), r=[wk, "silrep"], w=pk(b))
                    P.op("dve", lambda s=s, b=b: nc.vector.tensor_tensor(
                        out=gbs[:, s, :], in0=ps[:, b, 0:256], in1=brow[:, :], op=ALU.add), r=pk(b) + ["brow"], w=["gbs"])
                    c0 = (ci % 4) * 256
                    P.dma("sp", lambda s=s, gi=gi, c0=c0: nc.sync.dma_start(out=gbd[gi * NQ + s:gi * NQ + s + 1, c0:c0 + 256], in_=gbs[0:1, s, :]),
                          r=["gbs"], w=["gbd"])
        for (Gt, sht, sc_role, sh_role, lo) in ((G1T, sh1T, 1, 0, 0), (G2T, sh2T, 3, 2, 8)):
            P.op("dve", lambda Gt=Gt, sc_role=sc_role: nc.vector.tensor_scalar(
                out=Gt[:], in0=modT[:, sc_role, :, :], scalar1=1.0, scalar2=None, op0=ALU.add), r=["modT"], w=["GT"])
            P.op("dve", lambda Gt=Gt, lo=lo: nc.vector.tensor_tensor(
                out=Gt[:], in0=Gt[:], in1=lnT[:, lo:lo + 8].unsqueeze(2).to_broadcast([128, 8, NQ]), op=ALU.mult),
                 r=["lnT", "GT"], w=["GT"])
            P.op("dve", lambda sht=sht, sh_role=sh_role: nc.vector.tensor_copy(out=sht[:], in_=modT[:, sh_role, :, :]),
                 r=["modT"], w=["GT"])
        set_es.close()
        P.barrier()
        P.stage("ada")
        set_es = contextlib.ExitStack()

        ident_f = SBs("ident_f", [128, 128])
        bmask = SBs("bmask", [128, 128])
        rowmask = SBs("rowmask", [128, 2])
        P.dma("sp", lambda: nc.sync.dma_start(out=ident_f[:], in_=ident_in[:, :]), w=["ident_f"])
        P.dma("sp", lambda: nc.sync.dma_start(out=bmask[:], in_=bmask_in[:, :]), w=["bmask"])
        P.dma("sp", lambda: nc.sync.dma_start(out=rowmask[:], in_=rowmask_in[:, :]), w=["ssmsetup"])
        are = SBs("are", [128, 16]); aim = SBs("aim", [128, 16]); dtt = SBs("dtt", [128, 16])
        th = SBs("th", [128, 16]); rr = SBs("rr", [128, 16])
        cs = SBs("cs", [128, 16]); sn = SBs("sn", [128, 16]); t0 = SBs("t0", [128, 16]); t1 = SBs("t1", [128, 16])
        t2 = SBs("t2", [128, 16])
        Lre = SBs("Lre", [128, D + 1, 16]); Lim = SBs("Lim", [128, D + 1, 16])
        cre = SBs("cre", [128, 16]); cim = SBs("cim", [128, 16])
        Bre = SBs("Bre", [128, 16, 32]); Bim = SBs("Bim", [128, 16, 32])
        Bbre = SBs("Bbre", [128, 16, 32]); Bbim = SBs("Bbim", [128, 16, 32])
        Cx = SBs("Cx", [128, 2, 4, 128])
        Ctre = SBs("Ctre", [128, 16, 32]); Ctim = SBs("Ctim", [128, 16, 32])
        Vre = SBs("Vre", [128, D + 1, 16, 32]); Vim = SBs("Vim", [128, D + 1, 16, 32])
        wt1 = SBs("wt1", [128, 16, 32]); wt2 = SBs("wt2", [128, 16, 32])
        Wst = SBs("Wst", [128, 2, 16, 32])
        Dcol = SBs("Dcol", [128, 4])
        halfpi = SBs("halfpi", [128, 1])
        ST = ["ssmsetup"]
        with nc.allow_non_contiguous_dma(reason="tiny setup loads"):
            for e in range(2):
                P.dma("sp", lambda e=e: nc.sync.dma_start(out=are[64 * e:64 * e + 64, :], in_=A_re.rearrange("(a e) p -> e p a", e=2)[e]), w=ST)
                P.dma("sp", lambda e=e: nc.sync.dma_start(out=aim[64 * e:64 * e + 64, :], in_=A_im.rearrange("(a e) p -> e p a", e=2)[e]), w=ST)
                P.dma("sp", lambda e=e: nc.sync.dma_start(
                    out=dtt[64 * e:64 * e + 64, :], in_=log_dt.rearrange("(a e) -> e a", e=2)[e].partition_broadcast(64)), w=ST)
            P.dma("sp", lambda: nc.sync.dma_start(out=Dcol[:], in_=ssm_D.rearrange("(k p) -> p k", p=128)), w=ST)
        P.op("pool", lambda: nc.gpsimd.memset(Bre[:], 0.0), w=ST)
        P.op("pool", lambda: nc.gpsimd.memset(Bim[:], 0.0), w=ST)
        P.op("pool", lambda: nc.gpsimd.memset(Cx[:], 0.0), w=ST)
        P.op("pool", lambda: nc.gpsimd.memset(halfpi[:], math.pi / 2), w=ST)
        for e in range(2):
            for (Bt, Bsrc) in ((Bre, B_re), (Bim, B_im)):
                P.dma("sp", lambda e=e, Bt=Bt, Bsrc=Bsrc: nc.sync.dma_start(
                    out=Bt[64 * e:64 * e + 64, :, 16 * e:16 * e + 16],
                    in_=Bsrc.rearrange("(a e) p h -> e p a h", e=2)[e]), r=ST, w=ST)
            for ci_, Csrc in enumerate((C_re, C_im)):
                for ct in range(4):
                    for q in range(4):
                        P.dma("sp", lambda e=e, ci_=ci_, Csrc=Csrc, ct=ct, q=q: nc.sync.dma_start(
                            out=Cx[32 * q + 16 * e:32 * q + 16 * e + 16, ci_, ct, 64 * e:64 * e + 64],
                            in_=Csrc[8 * ct + 2 * q + e]), r=ST, w=ST)

        V = nc.vector

        def dv(fn):
            P.op("dve", fn, r=ST, w=ST)

        def cmul(ore, oim, are_, aim_, bre_, bim_, ta, tb):
            dv(lambda: V.tensor_tensor(out=ta, in0=are_, in1=bre_, op=ALU.mult))
            dv(lambda: V.tensor_tensor(out=tb, in0=aim_, in1=bim_, op=ALU.mult))
            dv(lambda: V.tensor_tensor(out=ore, in0=ta, in1=tb, op=ALU.subtract))
            dv(lambda: V.tensor_tensor(out=ta, in0=are_, in1=bim_, op=ALU.mult))
            dv(lambda: V.tensor_tensor(out=tb, in0=aim_, in1=bre_, op=ALU.mult))
            dv(lambda: V.tensor_tensor(out=oim, in0=ta, in1=tb, op=ALU.add))

        P.op("act", lambda: nc.scalar.activation(out=dtt[:], in_=dtt[:], func=AF.Exp), r=ST, w=ST)
        dv(lambda: V.tensor_tensor(out=th[:], in0=aim[:], in1=dtt[:], op=ALU.mult))
        dv(lambda: V.tensor_tensor(out=rr[:], in0=are[:], in1=dtt[:], op=ALU.mult))
        P.op("act", lambda: nc.scalar.activation(out=sn[:], in_=th[:], func=AF.Sin, scale=1.0 / 16), r=ST, w=ST)
        P.op("act", lambda: nc.scalar.activation(out=cs[:], in_=th[:], func=AF.Sin, scale=1.0 / 16, bias=halfpi[:, 0:1]), r=ST, w=ST)
        for _ in range(4):
            dv(lambda: V.tensor_tensor(out=t0[:], in0=cs[:], in1=cs[:], op=ALU.mult))
            dv(lambda: V.tensor_tensor(out=t1[:], in0=sn[:], in1=sn[:], op=ALU.mult))
            dv(lambda: V.tensor_tensor(out=t2[:], in0=cs[:], in1=sn[:], op=ALU.mult))
            dv(lambda: V.tensor_tensor(out=cs[:], in0=t0[:], in1=t1[:], op=ALU.subtract))
            dv(lambda: V.tensor_scalar(out=sn[:], in0=t2[:], scalar1=2.0, scalar2=None, op0=ALU.mult))
        P.op("act", lambda: nc.scalar.activation(out=t0[:], in_=rr[:], func=AF.Exp), r=ST, w=ST)
        P.op("act", lambda: nc.scalar.activation(out=rDt[:], in_=rr[:], func=AF.Exp, scale=float(D)), r=ST, w=ST)
        dv(lambda: V.memset(Lre[:, 0, :], 1.0))
        dv(lambda: V.memset(Lim[:, 0, :], 0.0))
        dv(lambda: V.tensor_tensor(out=Lre[:, 1, :], in0=t0[:], in1=cs[:], op=ALU.mult))
        dv(lambda: V.tensor_tensor(out=Lim[:, 1, :], in0=t0[:], in1=sn[:], op=ALU.mult))
        for k in range(2, D + 1):
            cmul(Lre[:, k, :], Lim[:, k, :], Lre[:, k - 1, :], Lim[:, k - 1, :], Lre[:, 1, :], Lim[:, 1, :], t1[:], t2[:])
        dv(lambda: V.reciprocal(out=t0[:], in_=rDt[:]))
        dv(lambda: V.tensor_tensor(out=E1re[:], in0=Lre[:, D, :], in1=t0[:], op=ALU.mult))
        dv(lambda: V.tensor_tensor(out=E1im[:], in0=Lim[:, D, :], in1=t0[:], op=ALU.mult))
        dv(lambda: V.tensor_tensor(out=t0[:], in0=are[:], in1=are[:], op=ALU.mult))
        dv(lambda: V.tensor_tensor(out=t1[:], in0=aim[:], in1=aim[:], op=ALU.mult))
        dv(lambda: V.tensor_tensor(out=t0[:], in0=t0[:], in1=t1[:], op=ALU.add))
        dv(lambda: V.reciprocal(out=t0[:], in_=t0[:]))
        dv(lambda: V.tensor_scalar(out=cs[:], in0=Lre[:, 1, :], scalar1=-1.0, scalar2=None, op0=ALU.add))
        dv(lambda: V.tensor_tensor(out=t1[:], in0=cs[:], in1=are[:], op=ALU.mult))
        dv(lambda: V.tensor_tensor(out=t2[:], in0=Lim[:, 1, :], in1=aim[:], op=ALU.mult))
        dv(lambda: V.tensor_tensor(out=t1[:], in0=t1[:], in1=t2[:], op=ALU.add))
        dv(lambda: V.tensor_tensor(out=cre[:], in0=t1[:], in1=t0[:], op=ALU.mult))
        dv(lambda: V.tensor_tensor(out=t1[:], in0=Lim[:, 1, :], in1=are[:], op=ALU.mult))
        dv(lambda: V.tensor_tensor(out=t2[:], in0=cs[:], in1=aim[:], op=ALU.mult))
        dv(lambda: V.tensor_tensor(out=t1[:], in0=t1[:], in1=t2[:], op=ALU.subtract))
        dv(lambda: V.tensor_tensor(out=cim[:], in0=t1[:], in1=t0[:], op=ALU.mult))
        bc = lambda col: col.unsqueeze(2).to_broadcast([128, 16, 32])
        cmul(Bbre[:], Bbim[:], Bre[:], Bim[:], bc(cre[:]), bc(cim[:]), wt1[:], wt2[:])
        for ci_, Ct in enumerate((Ctre, Ctim)):
            for ct in range(4):
                b = bank()
                P.op("pe", lambda ci_=ci_, ct=ct, b=b: nc.tensor.transpose(ps[:, b, 0:128], Cx[:, ci_, ct, :], ident_f[:]),
                     r=ST + ["ident_f"], w=pk(b))
                P.op("dve", lambda Ct=Ct, ct=ct, b=b: V.tensor_copy(
                    out=Ct[:, 4 * ct:4 * ct + 4, :], in_=ps[:, b, 0:128].rearrange("p (q c) -> p q c", q=4)), r=pk(b) + ST, w=ST)
        for k in range(D + 1):
            lr = bc(Lre[:, k, :]); li = bc(Lim[:, k, :])
            dv(lambda lr=lr: V.tensor_tensor(out=wt1[:], in0=Ctre[:], in1=lr, op=ALU.mult))
            dv(lambda li=li: V.tensor_tensor(out=wt2[:], in0=Ctim[:], in1=li, op=ALU.mult))
            dv(lambda k=k: V.tensor_tensor(out=Vre[:, k, :, :], in0=wt1[:], in1=wt2[:], op=ALU.subtract))
            dv(lambda li=li: V.tensor_tensor(out=wt1[:], in0=Ctre[:], in1=li, op=ALU.mult))
            dv(lambda lr=lr: V.tensor_tensor(out=wt2[:], in0=Ctim[:], in1=lr, op=ALU.mult))
            dv(lambda: V.tensor_tensor(out=wt1[:], in0=wt1[:], in1=wt2[:], op=ALU.add))
            dv(lambda k=k: V.tensor_scalar(out=Vim[:, k, :, :], in0=wt1[:], scalar1=-1.0, scalar2=None, op0=ALU.mult))
        P.op("pool", lambda: nc.gpsimd.memset(Vtab[:], 0.0), w=ST)
        for j in range(D):
            for c_, Vs in enumerate((Vre, Vim)):
                for par in range(2):
                    dv(lambda j=j, c_=c_, Vs=Vs, par=par: V.tensor_copy(
                        out=Vtab[:, par::2, j, c_, 32 * par:32 * par + 32], in_=Vs[:, j + 1, par::2, :]))
        for j in range(D):
            k = D - 1 - j
            cmul(Wst[:, 0, :, :], Wst[:, 1, :, :], Bbre[:], Bbim[:], bc(Lre[:, k, :]), bc(Lim[:, k, :]), wt1[:], wt2[:])
            for c_ in range(2):
                for ct in range(4):
                    b = bank()
                    P.op("pe", lambda c_=c_, ct=ct, b=b: nc.tensor.transpose(
                        ps[:, b, 0:128], Wst[:, c_, 4 * ct:4 * ct + 4, :].rearrange("p q c -> p (q c)"), ident_f[:]),
                         r=ST + ["ident_f"], w=pk(b))
                    for h in range(2):
                        for q2 in range(2):
                            P.op("dve", lambda c_=c_, ct=ct, b=b, h=h, q2=q2, j=j: V.tensor_scalar(
                                out=Wtab[64 * h:64 * h + 64, ct, j, c_, q2, :], in0=ps[64 * h:64 * h + 64, b, 0:128],
                                scalar1=rowmask[64 * h:64 * h + 64, q2:q2 + 1], scalar2=None, op0=ALU.mult),
                                 r=pk(b) + ST, w=ST)
        for d_ in range(D):
            for ct in range(4):
                b = bank()
                for qq in range(4):
                    for c_, (Bb, Vs) in enumerate(((Bbre, Vre), (Bbim, Vim))):
                        a = 4 * ct + qq
                        P.op("pe", lambda b=b, Bb=Bb, Vs=Vs, a=a, qq=qq, d_=d_, c_=c_: nc.tensor.matmul(
                            ps[:, b, 32 * qq:32 * qq + 32], lhsT=Bb[:, 4 * (a // 4):4 * (a // 4) + 4, :].rearrange("p q c -> p (q c)"),
                            rhs=Vs[:, d_, a, :], start=(c_ == 0), stop=(c_ == 1)), r=ST, w=pk(b))
                P.op("dve", lambda b=b, ct=ct, d_=d_: V.tensor_tensor(out=Ktab[:, ct, d_, :], in0=ps[:, b, 0:128], in1=bmask[:], op=ALU.mult),
                     r=pk(b) + ["bmask"] + ST, w=ST)
        for ct in range(4):
            dv(lambda ct=ct: V.scalar_tensor_tensor(out=Ktab[:, ct, 0, :], in0=ident_f[:], scalar=Dcol[:, ct:ct + 1],
                                                   in1=Ktab[:, ct, 0, :], op0=ALU.mult, op1=ALU.add))
        dv(lambda: V.memset(Ere[:, :, 0:1], 1.0))
        dv(lambda: V.memset(Eim[:, :, 0:1], 0.0))
        dv(lambda: V.tensor_copy(out=Ere[:, :, 1], in_=E1re[:]))
        dv(lambda: V.tensor_copy(out=Eim[:, :, 1], in_=E1im[:]))
        n = 2
        while n < 64:
            cmul(Ere[:, :, n], Eim[:, :, n], Ere[:, :, n - 1], Eim[:, :, n - 1], Ere[:, :, 1], Eim[:, :, 1], t1[:], t2[:])
            bn = lambda col, n=n: col.unsqueeze(2).to_broadcast([128, 16, n - 1])
            cmul(Ere[:, :, n + 1:2 * n], Eim[:, :, n + 1:2 * n], Ere[:, :, 1:n], Eim[:, :, 1:n],
                 bn(Ere[:, :, n]), bn(Eim[:, :, n]), wt1[:, :, 0:n - 1], wt2[:, :, 0:n - 1])
            n *= 2

        set_es.close()
        P.barrier()
        P.stage("ssmtab")
        p1w_es = contextlib.ExitStack()
        SB = lambda name, shape, dt=F32: p1w_es.enter_context(nc.sbuf_tensor("sb_" + name, list(shape), dt))
        sq = SB("sq", [128, 640], BF16)
        ssqk = SB("ssqk", [128, 10]); rsqk = SB("rsqk", [128, 10])
        qn = SB("qn", [128, 640])
        ra = SB("ra", [128, 10, 32]); rb = SB("rb", [128, 10, 32])
        qr = SB("qr", [128, 512], BF16)
        kr = SB("kr", [128, 128])
        krb = SB("krb", [128, 128], BF16)
        vf = SB("vf", [128, 128])
        qT2 = [SB("qT%d" % i, [128, 4, 256], BF16) for i in range(2)]
        kT3 = [SB("kT%d" % i, [128, 2, 128], BF16) for i in range(3)]
        Vaug3 = [SB("Vaug%d" % i, [128, 2, 2, 65], BF16) for i in range(3)]
        PT = SB("PT", [128, 2, 512], BF16)
        den = SB("den", [128, 8])
        attn = SB("attn", [128, 512])
        attnb = SB("attnb", [128, 512], BF16)
        ssa = SB("ssa", [128, 1]); rsa = SB("rsa", [128, 1])
        uT2 = [SB("uT%d" % i, [128, 4, 256], BF16) for i in range(2)]
        sA = SB("sA", [128, 16, 64]); sB = SB("sB", [128, 16, 64]); sC = SB("sC", [128, 16, 64]); sD = SB("sD", [128, 16, 64])
        Hbf2 = [SB("Hbf%d" % i, [128, 2, 16, 65], BF16) for i in range(2)]
        gT = SB("gT", [128, 4, 256], BF16)
        sig = SB("sig", [128, 256])
        soT = SB("soT", [128, 4, 256], BF16)
        sqT = SB("sqT", [128, 4, 256], BF16)
        rsb = SB("rsb", [128, 256])
        mTb = SB("mTb", [128, 8, 256], BF16)
        hinit = SB("hinit", [128, 2, 16])
        hst = SB("hst", [128, 16, 2])

        for i in range(3):
            P.op("pool", lambda i=i: nc.gpsimd.memset(Vaug3[i][:], 1.0), w=["Vaug%d" % i])

        seqs = []
        for s in range(NSEQ):
            seqs.append(dict(idx=s, T=S, prompt=True, row0=s * S))
        seqs.append(dict(idx=NSEQ, T=TS, prompt=False, row0=NSEQ * S))

        def xsrc(sq_, t0_, n):
            if sq_["prompt"]:
                return xp[sq_["idx"], t0_:t0_ + n, :]
            return xs[t0_:t0_ + n, :]

        def tok_view(ap2d, nt):
            if nt >= 128:
                return ap2d.rearrange("(t p) d -> p t d", p=128)
            return ap2d.unsqueeze(1)

        def norm_transpose(sq_, T, nt, TT, GT_, shT_, Xk, xo=0, ho=0, hk=("hT0", "hT1")):
            Xk = list(Xk) if isinstance(Xk, (list, tuple)) else [Xk]
            hk = list(hk) if isinstance(hk, (list, tuple)) else [hk]
            s = sq_["idx"]
            for tt in range(TT):
                P.op("act", lambda tt=tt: nc.scalar.activation(out=junk[0:nt, :], in_=X[0:nt, xo + tt, :], func=AF.Square,
                                                               accum_out=ss4[0:nt, tt:tt + 1]), r=Xk, w=["junk", "ss4"])
            P.op("act", lambda: nc.scalar.activation(out=rs4[0:nt, 0:TT], in_=ss4[0:nt, 0:TT], func=AF.Sqrt, scale=1.0 / DM, bias=epsc[0:nt, :]),
                 r=["ss4", "epsc"], w=["rs4"])
            P.op("dve", lambda: nc.vector.reciprocal(out=rs4[0:nt, 0:TT], in_=rs4[0:nt, 0:TT]), r=["rs4"], w=["rs4"])
            for half in range((TT + 1) // 2):
                tts = list(range(2 * half, min(TT, 2 * half + 2)))
                for tt in tts:
                    P.op("act", lambda tt=tt: nc.scalar.activation(out=xn[0:nt, tt % 2, :], in_=X[0:nt, xo + tt, :], func=AF.Copy,
                                                                   scale=rs4[0:nt, tt:tt + 1]), r=Xk + ["rs4"], w=["xn"])
                w0 = 2 * half * 128
                wn = len(tts) * nt if nt < 128 else len(tts) * 128
                for k in range(8):
                    b = bank()
                    for tt in tts:
                        P.op("pe", lambda tt=tt, k=k, b=b: nc.tensor.transpose(
                            psb[b][:, (tt % 2) * 128:(tt % 2) * 128 + nt], xn[0:nt, tt % 2, k * 128:(k + 1) * 128], ident_b[0:nt, 0:nt]),
                             r=["xn", "ident_b"], w=pk(b))
                    P.op("dve", lambda k=k, b=b, w0=w0, wn=wn: nc.vector.tensor_scalar(
                        out=hT[:, k, ho + w0:ho + w0 + wn], in0=psb[b][:, 0:wn], scalar1=GT_[:, k, s:s + 1], scalar2=shT_[:, k, s:s + 1],
                        op0=ALU.mult, op1=ALU.add), r=pk(b) + ["GT"], w=hk)


        def mixer_front(sq_, blk, T, gb):
            P.stage("mix_%d_%d" % (sq_["idx"], blk))
            par = gb % 2
            xo = 2 * par
            ho = 256 * par
            hk = "hT%d" % par
            qT = qT2[par]; qk_ = "qT%d" % par
            uT = uT2[par]; uk_ = "uT%d" % par
            kTo = kT3[gb % 3]; Vo = Vaug3[gb % 3]; kk_ = "kT%d" % (gb % 3); vk_ = "Vaug%d" % (gb % 3)
            Hbf = Hbf2[par]; hbk = "Hbf%d" % par
            s = sq_["idx"]
            prompt = sq_["prompt"]
            t0_ = blk * T
            nt = min(T, 128)
            TT = max(1, T // 128)
            first = (blk == 0)
            last = (t0_ + T == sq_["T"])
            Nc = T // D
            Xk = "X%d" % par
            P.dma("sp", lambda: nc.sync.dma_start(out=X[0:nt, xo:xo + TT, :], in_=tok_view(xsrc(sq_, t0_, T), nt)), w=[Xk])
            norm_transpose(sq_, T, nt, TT, G1T, sh1T, Xk, xo=xo, ho=ho, hk=hk)
            P.stage("m_a")
            for tt in range(TT):
                bq = bank(); bkv = bank()
                for k in range(8):
                    P.op("pe", lambda k=k, tt=tt, bq=bq: nc.tensor.matmul(
                        ps[0:nt, bq, :], lhsT=hT[:, k, ho + tt * 128:ho + tt * 128 + nt], rhs=Win[:, k, 0:512], start=(k == 0), stop=(k == 7)),
                         r=[hk, "Win"], w=pk(bq))
                    P.op("pe", lambda k=k, tt=tt, bkv=bkv: nc.tensor.matmul(
                        ps[0:nt, bkv, 0:256], lhsT=hT[:, k, ho + tt * 128:ho + tt * 128 + nt], rhs=Win[:, k, 512:768], start=(k == 0), stop=(k == 7)),
                         r=[hk, "Win"], w=pk(bkv))
                P.op("act", lambda bq=bq: nc.scalar.activation(out=sq[0:nt, 0:512], in_=ps[0:nt, bq, :], func=AF.Square), r=pk(bq), w=["sq"])
                P.op("act", lambda bkv=bkv: nc.scalar.activation(out=sq[0:nt, 512:640], in_=ps[0:nt, bkv, 0:128], func=AF.Square), r=pk(bkv), w=["sq"])
                P.op("dve", lambda: nc.vector.tensor_reduce(out=ssqk[0:nt, :], in_=sq[0:nt, :].rearrange("p (h d) -> p h d", d=HD),
                                                            axis=AX.X, op=ALU.add), r=["sq"], w=["ssqk"])
                P.op("act", lambda: nc.scalar.activation(out=rsqk[0:nt, :], in_=ssqk[0:nt, :], func=AF.Sqrt, scale=1.0 / HD, bias=epsc[0:nt, :]),
                     r=["ssqk", "epsc"], w=["rsqk"])
                P.op("dve", lambda: nc.vector.reciprocal(out=rsqk[0:nt, :], in_=rsqk[0:nt, :]), r=["rsqk"], w=["rsqk"])
                P.op("dve", lambda bq=bq: nc.vector.tensor_tensor(
                    out=qn[0:nt, 0:512].rearrange("p (h d) -> p h d", d=HD), in0=ps[0:nt, bq, :].rearrange("p (h d) -> p h d", d=HD),
                    in1=rsqk[0:nt, 0:8].unsqueeze(2).to_broadcast([nt, 8, HD]), op=ALU.mult), r=pk(bq) + ["rsqk"], w=["qn"])
                P.op("dve", lambda bkv=bkv: nc.vector.tensor_tensor(
                    out=qn[0:nt, 512:640].rearrange("p (h d) -> p h d", d=HD), in0=ps[0:nt, bkv, 0:128].rearrange("p (h d) -> p h d", d=HD),
                    in1=rsqk[0:nt, 8:10].unsqueeze(2).to_broadcast([nt, 2, HD]), op=ALU.mult), r=pk(bkv) + ["rsqk"], w=["qn"])
                P.op("pool", lambda: nc.gpsimd.tensor_tensor(out=qn[0:nt, :], in0=qn[0:nt, :], in1=gqk[0:nt, :], op=ALU.mult),
                     r=["qn", "gqk"], w=["qn"])
                if prompt:
                    tix = (t0_ // 128) + tt
                    cosb = cosp[0:nt, tix, :]; sinb = sinp[0:nt, tix, :]
                else:
                    cosb = coss[0:nt, :]; sinb = sins[0:nt, :]
                cb = cosb.unsqueeze(1).to_broadcast([nt, 10, 32]); sbb = sinb.unsqueeze(1).to_broadcast([nt, 10, 32])
                q3 = qn[0:nt, :].rearrange("p (h x f) -> p h x f", x=2, f=32)
                x1 = q3[:, :, 0, :]; x2 = q3[:, :, 1, :]
                qr3 = qr[0:nt, :].rearrange("p (h x f) -> p h x f", x=2, f=32)
                kr3 = kr[0:nt, :].rearrange("p (h x f) -> p h x f", x=2, f=32)
                P.op("pool", lambda x1=x1, cb=cb: nc.gpsimd.tensor_tensor(out=ra[0:nt], in0=x1, in1=cb, op=ALU.mult), r=["qn", "rope"], w=["ra"])
                P.op("pool", lambda x2=x2, sbb=sbb: nc.gpsimd.tensor_tensor(out=rb[0:nt], in0=x2, in1=sbb, op=ALU.mult), r=["qn", "rope"], w=["rb"])
                P.op("dve", lambda qr3=qr3: nc.vector.tensor_tensor(out=qr3[:, :, 0, :], in0=ra[0:nt, 0:8], in1=rb[0:nt, 0:8], op=ALU.subtract),
                     r=["ra", "rb"], w=["qr"])
                P.op("dve", lambda kr3=kr3: nc.vector.tensor_tensor(out=kr3[:, :, 0, :], in0=ra[0:nt, 8:10], in1=rb[0:nt, 8:10], op=ALU.subtract),
                     r=["ra", "rb"], w=["kr"])
                P.op("pool", lambda x2=x2, cb=cb: nc.gpsimd.tensor_tensor(out=ra[0:nt], in0=x2, in1=cb, op=ALU.mult), r=["qn", "rope"], w=["ra"])
                P.op("pool", lambda x1=x1, sbb=sbb: nc.gpsimd.tensor_tensor(out=rb[0:nt], in0=x1, in1=sbb, op=ALU.mult), r=["qn", "rope"], w=["rb"])
                P.op("dve", lambda qr3=qr3: nc.vector.tensor_tensor(out=qr3[:, :, 1, :], in0=ra[0:nt, 0:8], in1=rb[0:nt, 0:8], op=ALU.add),
                     r=["ra", "rb"], w=["qr"])
                P.op("dve", lambda kr3=kr3: nc.vector.tensor_tensor(out=kr3[:, :, 1, :], in0=ra[0:nt, 8:10], in1=rb[0:nt, 8:10], op=ALU.add),
                     r=["ra", "rb"], w=["kr"])
                P.op("act", lambda: nc.scalar.copy(out=krb[0:nt, :], in_=kr[0:nt, :]), r=["kr"], w=["krb"])
                P.op("act", lambda bkv=bkv, tt=tt: nc.scalar.copy(
                    out=Vo[0:nt, tt, :, 0:64], in_=ps[0:nt, bkv, 128:256].rearrange("p (g d) -> p g d", g=2)), r=pk(bkv), w=[vk_])
                need_out = (not prompt) or (last and tt == TT - 1)
                if need_out:
                    P.op("act", lambda bkv=bkv: nc.scalar.copy(out=vf[0:nt, :], in_=ps[0:nt, bkv, 128:256]), r=pk(bkv), w=["vf"])
                    if prompt:
                        P.dma("sp", lambda: nc.sync.dma_start(out=kp[s, :, :], in_=kr[:, :]), r=["kr"])
                        P.dma("sp", lambda: nc.sync.dma_start(out=vp[s, :, :], in_=vf[:, :]), r=["vf"])
                    else:
                        P.dma("sp", lambda: nc.sync.dma_start(out=ks[:, :], in_=kr[0:nt, :]), r=["kr"])
                        P.dma("sp", lambda: nc.sync.dma_start(out=vs[:, :], in_=vf[0:nt, :]), r=["vf"])
                b = bank()
                for j in range(4):
                    P.op("pe", lambda j=j, b=b: nc.tensor.transpose(psb[b][:, j * 128:j * 128 + nt], qr[0:nt, j * 128:(j + 1) * 128], ident_b[0:nt, 0:nt]),
                         r=["qr", "ident_b"], w=pk(b))
                P.op("act", lambda b=b, tt=tt: nc.scalar.copy(
                    out=qT[:, :, tt * 128:tt * 128 + nt], in_=psb[b][:, 0:512].rearrange("p (j t) -> p j t", j=4)[:, :, 0:nt]), r=pk(b), w=[qk_])
                b2 = bank()
                P.op("pe", lambda b2=b2: nc.tensor.transpose(psb[b2][:, 0:nt], krb[0:nt, :], ident_b[0:nt, 0:nt]), r=["krb", "ident_b"], w=pk(b2))
                P.op("dve", lambda b2=b2, tt=tt: nc.vector.tensor_copy(out=kTo[:, tt, 0:nt], in_=psb[b2][:, 0:nt]), r=pk(b2), w=[kk_])
            P.stage("m_b")
            for ct in range(4):
                b = bank()
                for k in range(8):
                    P.op("pe", lambda k=k, ct=ct, b=b: nc.tensor.matmul(
                        ps[:, b, 0:T], lhsT=Win[:, k, 768 + ct * 128:768 + (ct + 1) * 128], rhs=hT[:, k, ho:ho + T], start=(k == 0), stop=(k == 7)),
                         r=[hk, "Win"], w=pk(b))
                P.op("act", lambda ct=ct, b=b: nc.scalar.copy(out=uT[:, ct, 0:T], in_=ps[:, b, 0:T]), r=pk(b), w=[uk_])

            P.stage("m_d")
            if first:
                if prompt:
                    P.op("dve", lambda: nc.vector.memset(Hre[:], 0.0), w=["H"])
                    P.op("dve", lambda: nc.vector.memset(Him[:], 0.0), w=["H"])
                else:
                    with nc.allow_non_contiguous_dma(reason="state load"):
                        for e in range(2):
                            P.dma("sp", lambda e=e: nc.sync.dma_start(out=hst[64 * e:64 * e + 64], in_=st0.rearrange("(a e) p c -> e p a c", e=2)[e]), w=["hst"])
                    P.op("dve", lambda: nc.vector.tensor_copy(out=Hre[:], in_=hst[:, :, 0]), r=["hst"], w=["H"])
                    P.op("dve", lambda: nc.vector.tensor_copy(out=Him[:], in_=hst[:, :, 1]), r=["hst"], w=["H"])
            bSS = bank(4)
            for c_ in range(2):
                for a in range(16):
                    ct, qq = a // 4, a % 4
                    h, q2 = qq // 2, qq % 2
                    bb = bSS + 2 * c_ + h
                    off = (ct * 2 + q2) * 64
                    for j in range(D):
                        P.op("pe", lambda c_=c_, ct=ct, h=h, q2=q2, bb=bb, off=off, j=j: nc.tensor.matmul(
                            ps[:, bb, off:off + Nc], lhsT=Wtab[64 * h:64 * h + 64, ct, j, c_, q2, :], rhs=uT[64 * h:64 * h + 64, ct, j:T:D],
                            start=(j == 0), stop=(j == D - 1)), r=[uk_, "Wtab"], w=pk(bb))
            kS = pk(bSS, 4)
            er = Ere[:, :, 0:Nc]; ei = Eim[:, :, 0:Nc]
            a_ = sA[:, :, 0:Nc]; b_ = sB[:, :, 0:Nc]; c__ = sC[:, :, 0:Nc]; d_ = sD[:, :, 0:Nc]
            hv = lambda t, h: t[:, :, 0:Nc].rearrange("p (ct h q2) n -> p h ct q2 n", ct=4, h=2)[:, h]
            pv = lambda c_, h: ps[:, bSS + 2 * c_ + h, :].rearrange("p (ct q2 n) -> p ct q2 n", ct=4, q2=2)[:, :, :, 0:Nc]
            for h in range(2):
                P.op("dve", lambda h=h: V.tensor_tensor(out=hv(sA, h), in0=pv(0, h), in1=hv(Ere, h), op=ALU.mult), r=kS + ["Etab"], w=["sA"])
                P.op("dve", lambda h=h: V.tensor_tensor(out=hv(sB, h), in0=pv(1, h), in1=hv(Eim, h), op=ALU.mult), r=kS + ["Etab"], w=["sB"])
                P.op("dve", lambda h=h: V.tensor_tensor(out=hv(sC, h), in0=pv(1, h), in1=hv(Ere, h), op=ALU.mult), r=kS + ["Etab"], w=["sC"])
                P.op("dve", lambda h=h: V.tensor_tensor(out=hv(sD, h), in0=pv(0, h), in1=hv(Eim, h), op=ALU.mult), r=kS + ["Etab"], w=["sD"])
            P.op("pool", lambda: nc.gpsimd.tensor_tensor(out=a_, in0=a_, in1=b_, op=ALU.add), r=["sA", "sB"], w=["sA"])
            P.op("pool", lambda: nc.gpsimd.tensor_tensor(out=c__, in0=c__, in1=d_, op=ALU.subtract), r=["sC", "sD"], w=["sC"])
            P.op("dve", lambda: V.tensor_tensor(out=hinit[:, 0, :], in0=E1re[:], in1=Hre[:], op=ALU.mult), r=["H", "Etab"], w=["hinit"])
            P.op("dve", lambda: V.tensor_tensor(out=hinit[:, 1, :], in0=E1im[:], in1=Him[:], op=ALU.mult), r=["H", "Etab"], w=["hinit"])
            P.op("dve", lambda: V.tensor_tensor(out=hinit[:, 0, :], in0=hinit[:, 0, :], in1=hinit[:, 1, :], op=ALU.subtract), r=["hinit"], w=["hinit"])
            P.op("dve", lambda: V.tensor_tensor(out=hinit[:, 1, :], in0=E1re[:], in1=Him[:], op=ALU.mult), r=["H", "Etab"], w=["hinit"])
            P.op("dve", lambda: V.tensor_tensor(out=hst[:, :, 0], in0=E1im[:], in1=Hre[:], op=ALU.mult), r=["H", "Etab"], w=["hst"])
            P.op("dve", lambda: V.tensor_tensor(out=hinit[:, 1, :], in0=hinit[:, 1, :], in1=hst[:, :, 0], op=ALU.add), r=["hinit", "hst"], w=["hinit"])
            P.op("act", lambda: nc.scalar.copy(out=Hbf[:, 0, :, 0], in_=Hre[:]), r=["H"], w=[hbk])
            P.op("act", lambda: nc.scalar.copy(out=Hbf[:, 1, :, 0], in_=Him[:]), r=["H"], w=[hbk])
            for a in range(16):
                P.op("dve", lambda a=a: V.tensor_tensor_scan(out=sB[:, a, 0:Nc], data0=rDt[:, a:a + 1].to_broadcast([128, Nc]), data1=sA[:, a, 0:Nc],
                                                           initial=hinit[:, 0, a:a + 1], op0=ALU.mult, op1=ALU.add),
                     r=["sA", "hinit", "Etab"], w=["sB"])
                P.op("dve", lambda a=a: V.tensor_tensor_scan(out=sD[:, a, 0:Nc], data0=rDt[:, a:a + 1].to_broadcast([128, Nc]), data1=sC[:, a, 0:Nc],
                                                           initial=hinit[:, 1, a:a + 1], op0=ALU.mult, op1=ALU.add),
                     r=["sC", "hinit", "Etab"], w=["sD"])
            P.op("pool", lambda: nc.gpsimd.tensor_tensor(out=a_, in0=b_, in1=er, op=ALU.mult), r=["sB", "Etab"], w=["sA"])
            P.op("pool", lambda: nc.gpsimd.tensor_tensor(out=c__, in0=d_, in1=ei, op=ALU.mult), r=["sD", "Etab"], w=["sC"])
            P.op("dve", lambda: V.tensor_tensor(out=a_, in0=a_, in1=c__, op=ALU.subtract), r=["sA", "sC"], w=["sA"])
            P.op("pool", lambda: nc.gpsimd.tensor_tensor(out=c__, in0=d_, in1=er, op=ALU.mult), r=["sD", "Etab", "sA"], w=["sC"])
            P.op("pool", lambda: nc.gpsimd.tensor_tensor(out=d_, in0=b_, in1=ei, op=ALU.mult), r=["sB", "Etab"], w=["sD"])
            P.op("dve", lambda: V.tensor_tensor(out=c__, in0=c__, in1=d_, op=ALU.add), r=["sC", "sD"], w=["sC"])
            P.op("act", lambda: nc.scalar.copy(out=Hbf[:, 0, :, 1:1 + Nc], in_=a_), r=["sA"], w=[hbk])
            P.op("act", lambda: nc.scalar.copy(out=Hbf[:, 1, :, 1:1 + Nc], in_=c__), r=["sC"], w=[hbk])
            P.op("dve", lambda: V.tensor_copy(out=Hre[:], in_=sA[:, :, Nc - 1]), r=["sA"], w=["H"])
            P.op("dve", lambda: V.tensor_copy(out=Him[:], in_=sC[:, :, Nc - 1]), r=["sC"], w=["H"])
            if last:
                P.op("dve", lambda: V.tensor_copy(out=hst[:, :, 0], in_=Hre[:]), r=["H"], w=["hst"])
                P.op("dve", lambda: V.tensor_copy(out=hst[:, :, 1], in_=Him[:]), r=["H"], w=["hst"])
                dst = hp[s] if prompt else hs
                with nc.allow_non_contiguous_dma(reason="state store"):
                    for e in range(2):
                        P.dma("sp", lambda dst=dst, e=e: nc.sync.dma_start(out=dst.rearrange("(a e) p c -> e p a c", e=2)[e], in_=hst[64 * e:64 * e + 64]), r=["hst"])

        def mixer_back(sq_, blk, T, gb):
            par = gb % 2
            xo = 2 * par
            ho = 256 * par
            hk = "hT%d" % par
            Xk = "X%d" % par
            qT = qT2[par]; qk_ = "qT%d" % par
            uT = uT2[par]; uk_ = "uT%d" % par
            kTo = kT3[gb % 3]; Vo = Vaug3[gb % 3]; kk_ = "kT%d" % (gb % 3); vk_ = "Vaug%d" % (gb % 3)
            Hbf = Hbf2[par]; hbk = "Hbf%d" % par
            s = sq_["idx"]
            prompt = sq_["prompt"]
            t0_ = blk * T
            nt = min(T, 128)
            TT = max(1, T // 128)
            first = (blk == 0)
            last = (t0_ + T == sq_["T"])
            Nc = T // D
            hb = (gb - 1) % 3 if prompt else (gb + 1) % 3
            kTh = kT3[hb]; Vh = Vaug3[hb]; kkh = "kT%d" % hb; vkh = "Vaug%d" % hb
            if first:
                P.dma("sp", lambda: nc.sync.dma_start(out=Gb[:], in_=gbd[s:s + 1, :].partition_broadcast(128)), r=["gbd"], w=["Gb"])
                for k in range(8):
                    P.dma("pool", lambda k=k: nc.gpsimd.dma_start(out=Wout[:, k, :], in_=w_out[k * 128:(k + 1) * 128, :]), w=["Wout%d" % k])
                    P.op("dve", lambda k=k: nc.vector.tensor_tensor(out=Wout[:, k, :], in0=Wout[:, k, :], in1=Gb[:, :], op=ALU.mult),
                         r=["Gb", "Wout%d" % k], w=["Wout%d" % k])
            P.stage("m_c")
            if not prompt:
                P.dma("sp", lambda: nc.sync.dma_start(out=tmp[:, 0:128], in_=ck[:, :]), w=["tmp"])
                P.dma("sp", lambda: nc.sync.dma_start(out=tmp[:, 128:256], in_=cv[:, :]), w=["tmp"])
                P.op("dve", lambda: nc.vector.tensor_copy(out=krb[:, :], in_=tmp[:, 0:128]), r=["tmp"], w=["krb"])
                b2 = bank()
                P.op("pe", lambda b2=b2: nc.tensor.transpose(psb[b2][:, 0:128], krb[:, :], ident_b[:, :]), r=["krb", "ident_b"], w=pk(b2))
                P.op("dve", lambda b2=b2: nc.vector.tensor_copy(out=kTh[:, 1, :], in_=psb[b2][:, 0:128]), r=pk(b2), w=[kkh])
                P.op("dve", lambda: nc.vector.tensor_copy(out=Vh[:, 1, :, 0:64], in_=tmp[:, 128:256].rearrange("p (g d) -> p g d", g=2)),
                     r=["tmp"], w=[vkh])
            for tt in range(TT):
                has_prev = (not prompt) or (not first) or tt > 0
                use_mask = prompt
                for g in range(2):
                    bS = bank(2)
                    prev_t = (kTo, Vo, tt - 1, kk_, vk_) if tt > 0 else (kTh, Vh, 1, kkh, vkh)
                    kts = ([prev_t + (False,)] if has_prev else []) + [(kTo, Vo, tt, kk_, vk_, True)]
                    for slot_, (kTx, Vx, kt, kkx, vkx, own) in enumerate(kts):
                        nk = nt if own else 128
                        bb = bS + (1 if own else 0)
                        P.op("pe", lambda g=g, kt=kt, nk=nk, bb=bb, tt=tt, use_mask=use_mask, kTx=kTx: nc.tensor.matmul(
                            ps[0:nk, bb, 0:4 * nt].rearrange("p (j t) -> p j t", j=4),
                            lhsT=kTx[64 * g:64 * g + 64, kt, 0:nk], rhs=qT[64 * g:64 * g + 64, :, tt * 128:tt * 128 + nt],
                            start=True, stop=(not use_mask)), r=[kkx, qk_], w=pk(bb))
                        if use_mask:
                            mi = 1 if own else 0
                            P.op("pe", lambda mi=mi, bb=bb: nc.tensor.matmul(
                                ps[:, bb, :], lhsT=mrow[0:1, mi * 128:(mi + 1) * 128], rhs=mcol[0:1, mi * 128:(mi + 1) * 128].unsqueeze(1).to_broadcast([1, 4, 128]), start=False, stop=True),
                                 r=["mrow", "mcol"], w=pk(bb))
                        P.op("act", lambda nk=nk, bb=bb, own=own: nc.scalar.activation(
                            out=PT[0:nk, 1 if own else 0, 0:4 * nt], in_=ps[0:nk, bb, 0:4 * nt], func=AF.Exp), r=pk(bb), w=["PT"])
                    bO = bank()
                    for j in range(4):
                        for slot_, (kTx, Vx, kt, kkx, vkx, own) in enumerate(kts):
                            nk = nt if own else 128
                            P.op("pe", lambda j=j, kt=kt, nk=nk, own=own, g=g, bO=bO, slot_=slot_, kts=kts, Vx=Vx: nc.tensor.matmul(
                                ps[0:nt, bO, j * 65:(j + 1) * 65], lhsT=PT[0:nk, 1 if own else 0, j * nt:(j + 1) * nt], rhs=Vx[0:nk, kt, g, :],
                                start=(slot_ == 0), stop=(slot_ == len(kts) - 1)), r=["PT", vkx], w=pk(bO))
                    o3 = ps[0:nt, bO, 0:260].rearrange("p (j c) -> p j c", c=65)
                    P.op("dve", lambda o3=o3, g=g: nc.vector.tensor_tensor(out=den[0:nt, 4 * g:4 * g + 4], in0=o3[:, :, 64], in1=esink[0:nt, 4 * g:4 * g + 4],
                                                                         op=ALU.add), r=pk(bO) + ["esink"], w=["den"])
                    P.op("dve", lambda g=g: nc.vector.reciprocal(out=den[0:nt, 4 * g:4 * g + 4], in_=den[0:nt, 4 * g:4 * g + 4]), r=["den"], w=["den"])
                    P.op("dve", lambda o3=o3, g=g: nc.vector.tensor_tensor(
                        out=attn[0:nt, 256 * g:256 * g + 256].rearrange("p (j d) -> p j d", d=HD), in0=o3[:, :, 0:64],
                        in1=den[0:nt, 4 * g:4 * g + 4].unsqueeze(2).to_broadcast([nt, 4, HD]), op=ALU.mult), r=pk(bO) + ["den"], w=["attn"])
                P.op("act", lambda: nc.scalar.activation(out=junk[0:nt, 0:512], in_=attn[0:nt, :], func=AF.Square, accum_out=ssa[0:nt, :]),
                     r=["attn"], w=["junk", "ssa"])
                P.op("act", lambda: nc.scalar.activation(out=rsa[0:nt, :], in_=ssa[0:nt, :], func=AF.Sqrt, scale=1.0 / 512, bias=epsc[0:nt, :]),
                     r=["ssa", "epsc"], w=["rsa"])
                P.op("dve", lambda: nc.vector.reciprocal(out=rsa[0:nt, :], in_=rsa[0:nt, :]), r=["rsa"], w=["rsa"])
                P.op("dve", lambda: nc.vector.scalar_tensor_tensor(out=attnb[0:nt, :], in0=attn[0:nt, :], scalar=rsa[0:nt, 0:1], in1=gao[0:nt, :],
                                                                   op0=ALU.mult, op1=ALU.mult), r=["attn", "rsa", "gao"], w=["attnb"])
                b = bank()
                for k in range(4):
                    P.op("pe", lambda k=k, b=b: nc.tensor.transpose(psb[b][:, k * 128:k * 128 + nt], attnb[0:nt, k * 128:(k + 1) * 128], ident_b[0:nt, 0:nt]),
                         r=["attnb", "ident_b"], w=pk(b))
                P.op("act", lambda b=b, tt=tt: nc.scalar.copy(
                    out=mTb[:, 0:4, tt * 128:tt * 128 + nt], in_=psb[b][:, 0:512].rearrange("p (j t) -> p j t", j=4)[:, :, 0:nt]), r=pk(b), w=["mTb"])

            P.stage("m_e")
            for ct in range(4):
                b = bank()
                for j in range(D):
                    nmm = (j + 1) + 8
                    i_ = 0
                    for jp in range(j + 1):
                        P.op("pe", lambda ct=ct, j=j, jp=jp, b=b, i_=i_, nmm=nmm: nc.tensor.matmul(
                            ps[:, b, j:T:D], lhsT=Ktab[:, ct, j - jp, :], rhs=uT[:, ct, jp:T:D], start=(i_ == 0), stop=(i_ == nmm - 1)),
                             r=[uk_, "Ktab"], w=pk(b))
                        i_ += 1
                    for qq in range(4):
                        a = 4 * ct + qq
                        h = qq // 2
                        for c_ in range(2):
                            P.op("pe", lambda a=a, h=h, c_=c_, j=j, b=b, i_=i_, nmm=nmm, qq=qq: nc.tensor.matmul(
                                ps[64 * h:64 * h + 64, b, j:T:D], lhsT=Vtab[:, a, j, c_, :], rhs=Hbf[:, c_, a, 0:Nc],
                                start=(i_ == 0), stop=(qq in (1, 3) and c_ == 1), tile_position=(0, 64 * h)), r=[hbk, "Vtab"], w=pk(b))
                            i_ += 1
                P.op("act", lambda ct=ct, b=b: nc.scalar.activation(out=gT[:, ct, 0:T], in_=ps[:, b, 0:T], func=AF.Gelu_apprx_tanh), r=pk(b), w=["gT"])
            P.stage("m_f")
            for co in range(4):
                b = bank()
                for ct in range(4):
                    P.op("pe", lambda co=co, ct=ct, b=b: nc.tensor.matmul(
                        ps[:, b, 0:T], lhsT=Wglu[:, ct, co * 128:(co + 1) * 128], rhs=gT[:, ct, 0:T], start=(ct == 0), stop=(ct == 3)),
                         r=["gT", "Wglu"], w=pk(b))
                P.op("act", lambda co=co, b=b: nc.scalar.activation(out=sig[:, 0:T], in_=ps[:, b, 0:T], func=AF.Sigmoid, bias=glubT[:, co:co + 1]),
                     r=pk(b) + ["glubT"], w=["sig"])
                P.op("dve", lambda co=co: V.tensor_tensor(out=soT[:, co, 0:T], in0=gT[:, co, 0:T], in1=sig[:, 0:T], op=ALU.mult),
                     r=["gT", "sig"], w=["soT"])
                P.op("pool", lambda co=co: nc.gpsimd.tensor_tensor(out=sqT[:, co, 0:T], in0=soT[:, co, 0:T], in1=soT[:, co, 0:T], op=ALU.mult),
                     r=["soT"], w=["sqT"])
            b = bank()
            for ct in range(4):
                P.op("pe", lambda ct=ct, b=b: nc.tensor.matmul(ps[:, b, 0:T], lhsT=ones_b[:, :], rhs=sqT[:, ct, 0:T], start=(ct == 0), stop=(ct == 3)),
                     r=["sqT", "ones_b"], w=pk(b))
            P.op("act", lambda b=b: nc.scalar.activation(out=rsb[:, 0:T], in_=ps[:, b, 0:T], func=AF.Sqrt, scale=1.0 / SSMW, bias=epsc[:, :]),
                 r=pk(b) + ["epsc"], w=["rsb"])
            P.op("dve", lambda: V.reciprocal(out=rsb[:, 0:T], in_=rsb[:, 0:T]), r=["rsb"], w=["rsb"])
            for co in range(4):
                P.op("dve", lambda co=co: V.scalar_tensor_tensor(out=mTb[:, 4 + co, 0:T], in0=soT[:, co, 0:T], scalar=gsoT[:, co:co + 1], in1=rsb[:, 0:T],
                                                               op0=ALU.mult, op1=ALU.mult), r=["soT", "rsb", "gsoT"], w=["mTb"])
            P.stage("m_g")
            for tt in range(TT):
                bo = bank(2)
                Xr = Xr2[tt % 2]; xrk = "Xr%d" % (tt % 2)
                P.dma("sp", lambda tt=tt, Xr=Xr: nc.sync.dma_start(out=Xr[0:nt, :], in_=xsrc(sq_, t0_ + tt * 128, nt)), w=[xrk])
                for hf in range(2):
                    for k in range(8):
                        P.op("pe", lambda k=k, hf=hf, tt=tt, bo=bo: nc.tensor.matmul(
                            ps[0:nt, bo + hf, :], lhsT=mTb[:, k, tt * 128:tt * 128 + nt], rhs=Wout[:, k, hf * 512:(hf + 1) * 512],
                            start=(k == 0), stop=(k == 7)), r=["mTb", "Wout%d" % k], w=pk(bo + hf))
                P.op("dve", lambda bo=bo, Xr=Xr: V.tensor_tensor(out=Xr[0:nt, :], in0=ps[0:nt, bo:bo + 2, :].rearrange("p b n -> p (b n)"), in1=Xr[0:nt, :], op=ALU.add),
                     r=pk(bo, 2) + [xrk], w=[xrk])
                rr0 = sq_["row0"] + t0_ + tt * 128
                P.dma("sp", lambda rr0=rr0, Xr=Xr: nc.sync.dma_start(out=x1d[rr0:rr0 + nt, :], in_=Xr[0:nt, :]), r=[xrk], w=["x1d%d" % (rr0 // 512)])

        mT = hT
        for o in ():
            pass

        T1 = 256
        blocks = []
        for sq_ in seqs:
            if sq_["prompt"]:
                for blk in range(sq_["T"] // T1):
                    blocks.append((sq_, blk, T1))
            else:
                blocks.append((sq_, 0, sq_["T"]))
        mixer_front(*blocks[0], 0)
        for gb, blk_ in enumerate(blocks):
            if gb + 1 < len(blocks):
                mixer_front(*blocks[gb + 1], gb + 1)
            mixer_back(*blk_, gb)

        p1w_es.close()
        p1_es.close()
        P.barrier()
        P.stage("phase1")
        p2_es = contextlib.ExitStack()
        SB = lambda name, shape, dt=F32: p2_es.enter_context(nc.sbuf_tensor("sb_" + name, list(shape), dt))
        aT = SB("aT", [128, NFT, 512], BF16)
        sg = SB("sg", [128, 2, 512])

        Wg = SB("Wg", [128, 8, DFF], BF16)
        Wu = SB("Wu", [128, 8, DFF], BF16)
        Wd = SB("Wd", [128, NFT, DM], BF16)
        NQF = 4
        fq = [(q * NFT // NFT) for q in range(NQF)]
        qb = [0, 6, 12, 17, 22]
        def wkey(nm, f):
            for q in range(NQF):
                if qb[q] <= f < qb[q + 1]:
                    return "%s_q%d" % (nm, q)
        for q in range(NQF):
            c0, c1 = qb[q] * 128, qb[q + 1] * 128
            for k in range(8):
                P.dma("pool", lambda k=k, c0=c0, c1=c1: nc.gpsimd.dma_start(out=Wg[:, k, c0:c1], in_=w_gate[k * 128:(k + 1) * 128, c0:c1]),
                      w=["Wg_q%d" % q])
                P.dma("pool", lambda k=k, c0=c0, c1=c1: nc.gpsimd.dma_start(out=Wu[:, k, c0:c1], in_=w_up[k * 128:(k + 1) * 128, c0:c1]),
                      w=["Wu_q%d" % q])
        for f in range(NFT):
            P.dma("pool", lambda f=f: nc.gpsimd.dma_start(out=Wd[:, f, :], in_=w_down[f * 128:(f + 1) * 128, :]),
                  w=["Wd_%d" % f])

        def ffn_dims(sq_, blk, T):
            t0_ = blk * T
            return sq_["idx"], sq_["prompt"], t0_, min(T, 128), max(1, T // 128), sq_["row0"] + t0_

        def ffn_front(sq_, blk, T):
            s, prompt, t0_, nt, TT, r0 = ffn_dims(sq_, blk, T)
            Xk = ["X0", "X1"]
            xkeys = ["x1d%d" % i for i in range(r0 // 512, (r0 + T - 1) // 512 + 1)]
            P.dma("sp", lambda: nc.sync.dma_start(out=X[0:nt, 0:TT, :], in_=tok_view(x1d[r0:r0 + T, :], nt)), r=xkeys, w=Xk)
            norm_transpose(sq_, T, nt, TT, G2T, sh2T, Xk)

        def ffn_gateup(sq_, blk, T):
            s, prompt, t0_, nt, TT, r0 = ffn_dims(sq_, blk, T)
            if blk == 0:
                P.dma("sp", lambda: nc.sync.dma_start(out=Gb[:], in_=gbd[NQ + s:NQ + s + 1, :].partition_broadcast(128)), r=["gbd"], w=["Gb"])
            for f in range(NFT):
                bg = bank(); bu = bank()
                for k in range(8):
                    P.op("pe", lambda f=f, k=k, bg=bg: nc.tensor.matmul(ps[:, bg, 0:T], lhsT=Wg[:, k, f * 128:(f + 1) * 128], rhs=hT[:, k, 0:T],
                                                                      start=(k == 0), stop=(k == 7)), r=["hT0", "hT1", wkey("Wg", f)], w=pk(bg), n=T)
                for k in range(8):
                    P.op("pe", lambda f=f, k=k, bu=bu: nc.tensor.matmul(ps[:, bu, 0:T], lhsT=Wu[:, k, f * 128:(f + 1) * 128], rhs=hT[:, k, 0:T],
                                                                      start=(k == 0), stop=(k == 7)), r=["hT0", "hT1", wkey("Wu", f)], w=pk(bu), n=T)
                P.op("act", lambda bg=bg, f=f: nc.scalar.activation(out=sg[:, f % 2, 0:T], in_=ps[:, bg, 0:T], func=AF.Silu), r=pk(bg), w=["sg%d" % (f % 2)], n=T)
                P.op("dve", lambda f=f, bu=bu: V.tensor_tensor(out=aT[:, f, 0:T], in0=ps[:, bu, 0:T], in1=sg[:, f % 2, 0:T], op=ALU.mult),
                     r=pk(bu) + ["sg%d" % (f % 2)], w=["aT%d" % f], n=T)

        def ffn_down(sq_, blk, T):
            s, prompt, t0_, nt, TT, r0 = ffn_dims(sq_, blk, T)
            for tt in range(TT):
                bo = bank(2)
                Xr = Xr2[tt % 2]; xrk = "Xr%d" % (tt % 2)
                rr0 = r0 + tt * 128
                P.dma("sp", lambda rr0=rr0, Xr=Xr: nc.sync.dma_start(out=Xr[0:nt, :], in_=x1d[rr0:rr0 + nt, :]), r=["x1d%d" % (rr0 // 512)], w=[xrk])
                for hf in range(2):
                    for f in range(NFT):
                        P.op("pe", lambda f=f, hf=hf, tt=tt, bo=bo: nc.tensor.matmul(
                            ps[0:nt, bo + hf, :], lhsT=aT[:, f, tt * 128:tt * 128 + nt], rhs=Wd[:, f, hf * 512:(hf + 1) * 512],
                            start=(f == 0), stop=(f == NFT - 1)), r=["aT%d" % f, "Wd_%d" % f], w=pk(bo + hf), n=512)
                P.op("dve", lambda bo=bo: V.tensor_tensor(out=tmp[0:nt, :], in0=ps[0:nt, bo:bo + 2, :].rearrange("p b n -> p (b n)"), in1=Gb[0:nt, :], op=ALU.mult),
                     r=pk(bo, 2) + ["Gb"], w=["tmp"], n=1024)
                P.op("pool", lambda Xr=Xr: nc.gpsimd.tensor_tensor(out=Xr[0:nt, :], in0=Xr[0:nt, :], in1=tmp[0:nt, :], op=ALU.add),
                     r=["tmp", xrk], w=[xrk], n=1024)
                if prompt:
                    dst = yp[s, t0_ + tt * 128:t0_ + tt * 128 + nt, :]
                else:
                    dst = ys[t0_ + tt * 128:t0_ + tt * 128 + nt, :]
                P.dma("sp", lambda dst=dst, Xr=Xr: nc.sync.dma_start(out=dst, in_=Xr[0:nt, :]), r=[xrk])

        T2 = 512
        fblocks = []
        for sq_ in seqs:
            if sq_["prompt"]:
                for blk in range(sq_["T"] // T2):
                    fblocks.append((sq_, blk, T2))
            else:
                fblocks.append((sq_, 0, sq_["T"]))
        ffn_front(*fblocks[0])
        for i_, fb in enumerate(fblocks):
            ffn_gateup(*fb)
            if i_ + 1 < len(fblocks):
                ffn_front(*fblocks[i_ + 1])
            ffn_down(*fb)

        import os
        if os.environ.get("KSCHED", "1") == "1":
            P.schedule(K=int(os.environ.get("KSCHEDK", "3")))
        with nc.allow_non_contiguous_dma(reason="small strided setup/state transfers"):
            P.emit()
        build_program.stats = P.stats
        p2_es.close()
    return nc


def _consts(S, TS):
    half = HD // 2
    inv = (10000.0 ** (-np.arange(half, dtype=np.float32) * 2.0 / HD)).astype(np.float32)
    pos_p = np.arange(S, dtype=np.float32)[:, None] * inv[None, :]
    pos_s = (PAST + np.arange(TS, dtype=np.float32))[:, None] * inv[None, :]
    bm = np.kron(np.eye(4, dtype=np.float32), np.ones((32, 32), np.float32))
    mrow = np.zeros((2, 128), np.float32)
    mrow[0, 0:64] = 1.0
    mrow[1, 64:128] = 1.0
    mcol = np.zeros((2, 128), np.float32)
    mcol[0, 64:128] = -30000.0
    mcol[1, 0:64] = -30000.0
    rowmask = np.zeros((128, 2), np.float32)
    for p in range(128):
        rowmask[p, (p // 32) % 2] = 1.0
    return dict(ident=np.eye(128, dtype=np.float32), bmask=bm, rowmask=rowmask,
                cosp=np.cos(pos_p).astype(np.float32), sinp=np.sin(pos_p).astype(np.float32),
                coss=np.cos(pos_s).astype(np.float32), sins=np.sin(pos_s).astype(np.float32),
                mrow=mrow.reshape(1, 256), mcol=mcol.reshape(1, 256))


_W_NAMES = ["w_ada", "b_ada", "ln1_g", "w_in", "q_norm_g", "k_norm_g", "attn_sinks", "ssm_A_re", "ssm_A_im", "ssm_log_dt",
            "ssm_B_re", "ssm_B_im", "ssm_C_re", "ssm_C_im", "ssm_D", "ssm_glu_w", "ssm_glu_b", "attn_out_g", "ssm_out_g",
            "w_out", "ln2_g", "w_gate", "w_up", "w_down"]


def kernel(**inputs):
    x_prompt = np.asarray(inputs["x_prompt"], np.float32)
    x_sample = np.asarray(inputs["x_sample"], np.float32)
    B, S, _ = x_prompt.shape
    DB, TS, _ = x_sample.shape
    ncores = DB
    NSEQ = B // ncores
    nc = build_program(S, NSEQ, TS)
    cst = _consts(S, TS)
    wmap = {n: np.ascontiguousarray(np.asarray(inputs[n], np.float32)[0]) for n in _W_NAMES}
    in_maps = []
    for c in range(ncores):
        m = dict(wmap)
        m.update(cst)
        m["xp"] = np.ascontiguousarray(x_prompt[c * NSEQ:(c + 1) * NSEQ])
        m["xs"] = np.ascontiguousarray(x_sample[c])
        m["ck"] = np.ascontiguousarray(np.asarray(inputs["cache_k"], np.float32)[0, c].reshape(128, 128))
        m["cv"] = np.ascontiguousarray(np.asarray(inputs["cache_v"], np.float32)[0, c].reshape(128, 128))
        m["st0"] = np.ascontiguousarray(np.asarray(inputs["state_ssm"], np.float32)[0, c])
        m["c3"] = np.ascontiguousarray(np.concatenate(
            [np.asarray(inputs["c_prompt"], np.float32)[c * NSEQ:(c + 1) * NSEQ], np.asarray(inputs["c_sample"], np.float32)[c:c + 1]], axis=0))
        in_maps.append(m)
    res = run_bass_kernel_spmd(nc, in_maps, core_ids=list(range(ncores)))
    R = res.results
    y_prompt = np.concatenate([r["yp"] for r in R], axis=0)
    y_sample = np.stack([r["ys"] for r in R], axis=0)
    k_prompt = np.concatenate([r["kp"] for r in R], axis=0).reshape(1, B, 128, NKV, HD)
    v_prompt = np.concatenate([r["vp"] for r in R], axis=0).reshape(1, B, 128, NKV, HD)
    ssm_prompt = np.concatenate([r["hp"] for r in R], axis=0).reshape(1, B, NG, NP, 2)
    k_sample = np.stack([r["ks"] for r in R], axis=0).reshape(1, DB, TS, NKV, HD)
    v_sample = np.stack([r["vs"] for r in R], axis=0).reshape(1, DB, TS, NKV, HD)
    ssm_sample = np.stack([r["hs"] for r in R], axis=0).reshape(1, DB, NG, NP, 2)
    return (y_prompt.astype(np.float32), y_sample.astype(np.float32), k_prompt.astype(np.float32), v_prompt.astype(np.float32),
            ssm_prompt.astype(np.float32), k_sample.astype(np.float32), v_sample.astype(np.float32), ssm_sample.astype(np.float32))
```

```python
import math
import contextlib
import numpy as np
import concourse.bass as bass
import concourse.mybir as mybir
from concourse.bass_utils import run_bass_kernel_spmd

F32 = mybir.dt.float32
BF16 = mybir.dt.bfloat16
AF = mybir.ActivationFunctionType
ALU = mybir.AluOpType
AX = mybir.AxisListType

DM = 1024
NH = 8
HD = 64
NKV = 2
SSMW = 512
NG = 32
NP = 64
NCH = 16
DFF = 2816
INC = 1280
EPS = 1e-6
PAST = 1024
D = 4
NFT = DFF // 128


class Prog:
    NS = 8
    PARANOID = ("act", "dve", "pool")
    LOOKAHEAD = 48

    def __init__(self, nc, es):
        import os
        if os.environ.get("KPAR") is not None:
            self.PARANOID = tuple(x for x in os.environ["KPAR"].split(",") if x)
        self.nc = nc
        self.eng = {"pe": nc.tensor, "act": nc.scalar, "dve": nc.vector, "pool": nc.gpsimd, "sp": nc.sync}
        self.ops = []
        self.sem = {e: es.enter_context(nc.semaphore("s_" + e)) for e in ("pe", "act", "dve", "pool")}
        self.dsem = {q: [es.enter_context(nc.semaphore("d_%s%d" % (q, i))) for i in range(self.NS)]
                     for q in ("sp", "pool", "act")}

    mute = False

    def stage(self, name):
        import os
        if os.environ.get("KSTAGE") == name:
            self.mute = True

    def op(self, eng, fn, r=(), w=(), dma=False, n=None):
        if self.mute:
            return
        self.ops.append(dict(eng=eng, fn=fn, r=tuple(r), w=tuple(w), dma=dma, deps=set(), sig=False, n=n))

    def dma(self, q, fn, r=(), w=(), n=None):
        self.op(q, fn, r, w, dma=True, n=n)

    COST = {"pe": (0.03, 1 / 2400.0, 128), "act": (0.22, 1 / 1200.0, 256), "dve": (0.1, 1 / 960.0, 256),
            "pool": (0.2, 1 / 450.0, 256), "sp": (2.0, 0.0, 0)}

    @staticmethod
    def _auto_n(o):
        e = o["eng"]
        r = " ".join(o["r"]); w = " ".join(o["w"])
        has = lambda t, k: k in t
        if e == "pe":
            if has(r, "Wtab") or has(r, "Ktab") or has(r, "Vtab"):
                return 64
            if has(r, "Wout") or has(r, "Wg_") or has(r, "Wu_") or has(r, "Wd_") or has(r, "mrow"):
                return 512
            if has(r, "kT") and has(r, "qT"):
                return 512
            if has(r, "Win"):
                return 384
            if has(r, "Wglu") or has(r, "ones_b"):
                return 256
            if has(r, "PT"):
                return 65
            return 128
        if e == "act":
            for k, n in (("junk", 1024), ("xn", 1024), ("Hbf", 1024), ("sq", 512), ("PT", 512), ("qT", 512), ("mTb", 512), ("sg", 512),
                         ("uT", 256), ("gT", 256), ("sig", 256), ("rsb", 256), ("Vaug", 128)):
                if has(w, k):
                    return n
            return 64
        if e == "dve":
            if has(w, "rsb"):
                return 2048
            for k, n in (("tmp", 1024), ("attnb", 512), ("qn", 512), ("aT", 512), ("sA", 512), ("sB", 512), ("sC", 512), ("sD", 512),
                         ("hT", 256), ("qr", 256), ("attn", 256), ("soT", 256), ("mTb", 256), ("kr", 64)):
                if has(w, k):
                    return n
            return 64
        if e == "pool":
            for k, n in (("Xr", 1024), ("sA", 1024), ("sC", 1024), ("sD", 1024), ("qn", 640), ("ra", 320), ("rb", 320), ("sqT", 256)):
                if has(w, k):
                    return n
            return 128
        return 0

    def _cost(self, o):
        a, b, dn = self.COST.get(o["eng"], (0.3, 0.0, 0))
        if o["dma"]:
            return 2.0 if o["n"] is None else o["n"]
        n = o["n"] if o["n"] is not None else self._auto_n(o)
        return a + b * n

    SCHED_ENG = ("pe", "act", "dve", "pool", "sp")

    def schedule(self, K=6):
        import heapq, os
        if os.environ.get("KSCHEDE") is not None:
            self.SCHED_ENG = tuple(os.environ["KSCHEDE"].split(","))
        out = []
        seg = []
        for o in self.ops + [dict(barrier=True)]:
            if o.get("barrier"):
                out.extend(self._sched_seg(seg, K))
                out.append(o)
                seg = []
            else:
                seg.append(o)
        out.pop()
        self.ops = out

    def _sched_seg(self, ops, K):
        import heapq
        n = len(ops)
        if n == 0:
            return []
        lastw = {}
        readers = {}
        preds = [set() for _ in range(n)]
        for i, o in enumerate(ops):
            for k in o["r"]:
                if k in lastw:
                    preds[i].add(lastw[k])
            for k in o["w"]:
                if k in lastw:
                    preds[i].add(lastw[k])
                preds[i].update(readers.get(k, ()))
            for k in o["r"]:
                readers.setdefault(k, []).append(i)
            for k in o["w"]:
                lastw[k] = i
                readers[k] = []
            preds[i].discard(i)
        succ = [[] for _ in range(n)]
        indeg = [len(p) for p in preds]
        for i, p in enumerate(preds):
            for j in p:
                succ[j].append(i)
        avail = {}
        ready_t = [0.0] * n
        for i in range(n):
            if indeg[i] == 0:
                heapq.heappush(avail.setdefault(ops[i]["eng"], []), i)
        eng_free = {}
        start = [0.0] * n
        done = 0
        while done < n:
            best = None
            for e, hp in avail.items():
                if not hp:
                    continue
                cand = heapq.nsmallest(K if e in self.SCHED_ENG else 1, hp)
                ef = eng_free.get(e, 0.0)
                ci = min(cand, key=lambda i: (max(ef, ready_t[i]), i))
                st = max(ef, ready_t[ci])
                if best is None or (st, ci) < (best[0], best[1]):
                    best = (st, ci, e)
            st, i, e = best
            avail[e].remove(i)
            heapq.heapify(avail[e])
            o = ops[i]
            c = self._cost(o)
            start[i] = st
            if o["dma"]:
                eng_free[e] = st + 0.06
            else:
                eng_free[e] = st + c
            fin = st + c
            for j in succ[i]:
                indeg[j] -= 1
                if fin > ready_t[j]:
                    ready_t[j] = fin
                if indeg[j] == 0:
                    heapq.heappush(avail.setdefault(ops[j]["eng"], []), j)
            done += 1
        order = sorted(range(n), key=lambda i: (start[i], i))
        self.sim_time = getattr(self, "sim_time", 0.0) + max(start[i] + self._cost(ops[i]) for i in range(n))
        return [ops[i] for i in order]

    def barrier(self):
        self.ops.append(dict(barrier=True))

    def emit(self):
        raw = self.ops
        ops = []
        last_eng = {}
        recent_dma = {}
        bar_deps = None
        need = {}
        for o in raw:
            if o.get("barrier"):
                bar_deps = set(last_eng.values())
                for q, lst in recent_dma.items():
                    bar_deps.update(lst[-self.NS:])
                need = {e: True for e in self.eng}
                continue
            i = len(ops)
            ops.append(o)
            if need.get(o["eng"]):
                o["deps"].update(bar_deps)
                need[o["eng"]] = False
            last_eng[o["eng"]] = i
            if o["dma"]:
                recent_dma.setdefault(o["eng"], []).append(i)
        self.ops = ops
        lastw = {}
        readers = {}
        for i, o in enumerate(ops):
            for k in o["r"]:
                if k in lastw:
                    o["deps"].add(lastw[k])
            for k in o["w"]:
                if k in lastw:
                    o["deps"].add(lastw[k])
                for j in readers.get(k, ()):
                    o["deps"].add(j)
            for k in o["r"]:
                lst = readers.setdefault(k, [])
                if not o["dma"]:
                    lst[:] = [j for j in lst if ops[j]["dma"] or ops[j]["eng"] != o["eng"]]
                lst.append(i)
            for k in o["w"]:
                lastw[k] = i
                readers[k] = []
        for i, o in enumerate(ops):
            o["deps"].discard(i)
            for j in o["deps"]:
                p = ops[j]
                if p["dma"] or o["dma"] or p["eng"] != o["eng"] or p["eng"] in self.PARANOID:
                    p["sig"] = True
        cnt = {e: 0 for e in self.sem}
        dcnt = {q: 0 for q in self.dsem}
        waited = {}
        tot_wait = 0
        eng_ops = {}
        for i, o in enumerate(ops):
            eng_ops.setdefault(o["eng"], []).append(i)
        eng_pos = {e: 0 for e in eng_ops}
        for i, o in enumerate(ops):
            e = o["eng"]
            E = self.eng[e]
            pos = eng_pos[e]
            eng_pos[e] = pos + 1
            for j in sorted(o["deps"]):
                p = ops[j]
                if not p["sig"]:
                    continue
                if p["eng"] == e and not p["dma"] and not o["dma"] and e not in self.PARANOID:
                    continue
                s, v = p["sigval"]
                key = (e, id(s))
                if waited.get(key, 0) >= v:
                    continue
                for i2 in eng_ops[e][pos + 1:pos + 1 + self.LOOKAHEAD]:
                    for j2 in ops[i2]["deps"]:
                        if j2 < i and ops[j2]["sig"] and not ops[j2]["dma"] and "sigval" in ops[j2]:
                            s2, v2 = ops[j2]["sigval"]
                            if s2 is s and v2 > v:
                                v = v2
                E.wait_ge(s, v)
                tot_wait += 1
                waited[key] = v
            if o["dma"]:
                n = dcnt[e]
                s = self.dsem[e][n % self.NS]
                prev = 16 * (n // self.NS)
                if prev > 0:
                    key = (e, id(s))
                    if waited.get(key, 0) < prev:
                        E.wait_ge(s, prev)
                        waited[key] = prev
                ins = o["fn"]()
                ins.then_inc(s, 16)
                o["sigval"] = (s, prev + 16)
                o["sig"] = True
                dcnt[e] = n + 1
            else:
                ins = o["fn"]()
                if o["sig"]:
                    cnt[e] += 1
                    ins.then_inc(self.sem[e], 1)
                    o["sigval"] = (self.sem[e], cnt[e])
        sp = self.eng["sp"]
        for q in self.dsem:
            n = dcnt[q]
            for t in range(max(0, n - self.NS), n):
                sp.wait_ge(self.dsem[q][t % self.NS], 16 * (t // self.NS + 1))
        self.stats = dict(n_ops=len(ops), n_wait=tot_wait, sim_us=getattr(self, "sim_time", None))


def build_program(S, NSEQ=2, TS=32):
    nc = bass.Bass("TRN2", target_bir_lowering=False)
    dt_in = lambda name, shape: nc.dram_tensor(name, list(shape), F32, kind="ExternalInput").ap()
    dt_out = lambda name, shape: nc.dram_tensor(name, list(shape), F32, kind="ExternalOutput").ap()

    xp = dt_in("xp", [NSEQ, S, DM])
    xs = dt_in("xs", [TS, DM])
    ck = dt_in("ck", [128, 128])
    cv = dt_in("cv", [128, 128])
    st0 = dt_in("st0", [NG, NP, 2])
    c3 = dt_in("c3", [NSEQ + 1, DM])
    w_ada = dt_in("w_ada", [DM, 6 * DM])
    b_ada = dt_in("b_ada", [6 * DM])
    ln1_g = dt_in("ln1_g", [DM])
    w_in = dt_in("w_in", [DM, INC])
    q_norm_g = dt_in("q_norm_g", [HD])
    k_norm_g = dt_in("k_norm_g", [HD])
    attn_sinks = dt_in("attn_sinks", [NH])
    A_re = dt_in("ssm_A_re", [NG, NP])
    A_im = dt_in("ssm_A_im", [NG, NP])
    log_dt = dt_in("ssm_log_dt", [NG])
    B_re = dt_in("ssm_B_re", [NG, NP, NCH])
    B_im = dt_in("ssm_B_im", [NG, NP, NCH])
    C_re = dt_in("ssm_C_re", [NG, NCH, NP])
    C_im = dt_in("ssm_C_im", [NG, NCH, NP])
    ssm_D = dt_in("ssm_D", [SSMW])
    glu_w = dt_in("ssm_glu_w", [SSMW, SSMW])
    glu_b = dt_in("ssm_glu_b", [SSMW])
    attn_out_g = dt_in("attn_out_g", [512])
    ssm_out_g = dt_in("ssm_out_g", [SSMW])
    w_out = dt_in("w_out", [DM, DM])
    ln2_g = dt_in("ln2_g", [DM])
    w_gate = dt_in("w_gate", [DM, DFF])
    w_up = dt_in("w_up", [DM, DFF])
    w_down = dt_in("w_down", [DFF, DM])
    ident_in = dt_in("ident", [128, 128])
    bmask_in = dt_in("bmask", [128, 128])
    cosp_in = dt_in("cosp", [S, 32])
    sinp_in = dt_in("sinp", [S, 32])
    coss_in = dt_in("coss", [TS, 32])
    sins_in = dt_in("sins", [TS, 32])
    mrow_in = dt_in("mrow", [1, 256])
    mcol_in = dt_in("mcol", [1, 256])
    rowmask_in = dt_in("rowmask", [128, 2])

    yp = dt_out("yp", [NSEQ, S, DM])
    ys = dt_out("ys", [TS, DM])
    kp = dt_out("kp", [NSEQ, 128, 128])
    vp = dt_out("vp", [NSEQ, 128, 128])
    hp = dt_out("hp", [NSEQ, NG, NP, 2])
    ks = dt_out("ks", [TS, 128])
    vs = dt_out("vs", [TS, 128])
    hs = dt_out("hs", [NG, NP, 2])

    NTOK = NSEQ * S + TS
    x1d = nc.dram_tensor("x1d", [NTOK, DM], F32, kind="Internal").ap()
    gbd = nc.dram_tensor("gbd", [2 * (NSEQ + 1), DM], F32, kind="Internal").ap()

    es = contextlib.ExitStack()
    with es:
        P = Prog(nc, es)
        SB = lambda name, shape, dt=F32: es.enter_context(nc.sbuf_tensor("sb_" + name, list(shape), dt))
        ps = es.enter_context(nc.psum_tensor("ps", [128, 8, 512], F32))
        psb = [ps[:, b, :].bitcast(BF16) for b in range(8)]
        pctr = [0]

        def bank(n=1):
            b = pctr[0]
            if n > 1:
                b = (b + n - 1) // n * n
            if b + n > 8:
                b = 0
            pctr[0] = (b + n) % 8
            return b

        def pk(b, n=1):
            return ["ps%d" % (b + i) for i in range(n)]

        p1_es = contextlib.ExitStack()
        SB1 = lambda name, shape, dt=F32: p1_es.enter_context(nc.sbuf_tensor("sb_" + name, list(shape), dt))
        NQ = NSEQ + 1
        ident_b = SB("ident_b", [128, 128], BF16)
        epsc = SB("epsc", [128, 1])
        G1T = SB("G1T", [128, 8, NQ])
        sh1T = SB("sh1T", [128, 8, NQ])
        G2T = SB("G2T", [128, 8, NQ])
        sh2T = SB("sh2T", [128, 8, NQ])
        Gb = SB("Gb", [128, DM])
        X = SB("X", [128, 4, DM])
        xn = SB("xn", [128, 2, DM], BF16)
        junk = SB("junk", [128, DM], BF16)
        hT = SB("hT", [128, 8, 512], BF16)
        ss4 = SB("ss4", [128, 4]); rs4 = SB("rs4", [128, 4])
        tmp = SB("tmp", [128, DM])
        Xr2 = [SB("Xr%d" % i, [128, DM]) for i in range(2)]
        ones_b = SB1("ones_b", [128, 128], BF16)
        mrow = SB1("mrow", [1, 256], BF16)
        mcol = SB1("mcol", [1, 256], BF16)
        Win = SB1("Win", [128, 8, INC], BF16)
        Wout = SB1("Wout", [128, 8, DM], BF16)
        Wglu = SB1("Wglu", [128, 4, SSMW], BF16)
        glubT = SB1("glubT", [128, 4])
        gsoT = SB1("gsoT", [128, 4])
        gao = SB1("gao", [128, 512], BF16)
        g64 = SB1("g64", [128, 128])
        gqk = SB1("gqk", [128, 640])
        esink = SB1("esink", [128, 8])
        cosp = SB1("cosp", [128, S // 128, 32])
        sinp = SB1("sinp", [128, S // 128, 32])
        coss = SB1("coss", [128, 32])
        sins = SB1("sins", [128, 32])
        Wtab = SB1("Wtab", [128, 4, D, 2, 2, 128], BF16)
        Vtab = SB1("Vtab", [128, 16, D, 2, 64], BF16)
        Ktab = SB1("Ktab", [128, 4, D, 128], BF16)
        Ere = SB1("Ere", [128, 16, 64])
        Eim = SB1("Eim", [128, 16, 64])
        rDt = SB1("rDt", [128, 16])
        E1re = SB1("E1re", [128, 16])
        E1im = SB1("E1im", [128, 16])
        Hre = SB1("Hre", [128, 16])
        Him = SB1("Him", [128, 16])

        P.op("pool", lambda: nc.gpsimd.memset(epsc[:], EPS), w=["epsc"])
        P.dma("pool", lambda: nc.gpsimd.dma_start(out=ident_b[:], in_=ident_in[:, :]), w=["ident_b"])
        P.dma("pool", lambda: nc.gpsimd.dma_start(out=mrow[:], in_=mrow_in[:, :]), w=["mrow"])
        P.dma("pool", lambda: nc.gpsimd.dma_start(out=mcol[:], in_=mcol_in[:, :]), w=["mcol"])
        P.op("pool", lambda: nc.gpsimd.memset(ones_b[:], 1.0), w=["ones_b"])
        P.dma("sp", lambda: nc.sync.dma_start(out=cosp[:], in_=cosp_in.rearrange("(t p) f -> p t f", p=128)), w=["rope"])
        P.dma("sp", lambda: nc.sync.dma_start(out=sinp[:], in_=sinp_in.rearrange("(t p) f -> p t f", p=128)), w=["rope"])
        P.dma("sp", lambda: nc.sync.dma_start(out=coss[0:TS, :], in_=coss_in[:, :]), w=["rope"])
        P.dma("sp", lambda: nc.sync.dma_start(out=sins[0:TS, :], in_=sins_in[:, :]), w=["rope"])
        for k in range(8):
            for g in range(2):
                P.dma("pool", lambda k=k, g=g: nc.gpsimd.dma_start(
                    out=Win[:, k, 0:512].rearrange("p (j g d) -> p g j d", j=4, g=2)[:, g],
                    in_=w_in[k * 128:(k + 1) * 128, 256 * g:256 * g + 256].rearrange("p (j d) -> p j d", j=4)), w=["Win"])
            P.dma("pool", lambda k=k: nc.gpsimd.dma_start(out=Win[:, k, 512:INC], in_=w_in[k * 128:(k + 1) * 128, 512:INC]), w=["Win"])
            P.dma("pool", lambda k=k: nc.gpsimd.dma_start(out=Wout[:, k, :], in_=w_out[k * 128:(k + 1) * 128, :]), w=["Wout%d" % k])
        P.dma("pool", lambda: nc.gpsimd.dma_start(out=Wglu[:], in_=glu_w.rearrange("(k p) n -> p k n", p=128)), w=["Wglu"])
        P.dma("sp", lambda: nc.sync.dma_start(out=glubT[:], in_=glu_b.rearrange("(k p) -> p k", p=128)), w=["glubT"])
        P.dma("sp", lambda: nc.sync.dma_start(out=gsoT[:], in_=ssm_out_g.rearrange("(k p) -> p k", p=128)), w=["gsoT"])
        P.dma("pool", lambda: nc.gpsimd.dma_start(out=gao[:], in_=attn_out_g.partition_broadcast(128)), w=["gao"])
        P.dma("sp", lambda: nc.sync.dma_start(out=g64[:, 0:64], in_=q_norm_g.partition_broadcast(128)), w=["g64"])
        P.dma("sp", lambda: nc.sync.dma_start(out=g64[:, 64:128], in_=k_norm_g.partition_broadcast(128)), w=["g64"])
        P.op("dve", lambda: nc.vector.tensor_scalar(out=gqk[:, 0:512].rearrange("p (h d) -> p h d", h=8),
                                                    in0=g64[:, 0:64].unsqueeze(1).to_broadcast([128, 8, HD]),
                                                    scalar1=HD ** -0.5, scalar2=None, op0=ALU.mult), r=["g64"], w=["gqk"])
        P.op("dve", lambda: nc.vector.tensor_copy(out=gqk[:, 512:640].rearrange("p (h d) -> p h d", h=2),
                                                  in_=g64[:, 64:128].unsqueeze(1).to_broadcast([128, 2, HD])), r=["g64"], w=["gqk"])
        P.dma("sp", lambda: nc.sync.dma_start(out=esink[:], in_=attn_sinks.partition_broadcast(128)), w=["esink"])
        P.op("act", lambda: nc.scalar.activation(out=esink[:], in_=esink[:], func=AF.Exp), r=["esink"], w=["esink"])

        P.stage("const")
        set_es = contextlib.ExitStack()
        SBs = lambda name, shape, dt=F32: set_es.enter_context(nc.sbuf_tensor("sb_" + name, list(shape), dt))
        cT = SBs("cT", [128, 8, NQ])
        silT = SBs("silT", [128, 8, 4])
        silrep = SBs("silrep", [128, NQ, 8, 128])
        wch = [SBs("wch%d" % i, [128, 8, 256]) for i in range(4)]
        bT = SBs("bT", [128, 48])
        brow = SBs("brow", [128, 256])
        lnT = SBs("lnT", [128, 16])
        modT = SBs("modT", [128, 4, 8, NQ])
        gbs = SBs("gbs", [128, NQ, 256])
        for s_ in range(NQ):
            P.dma("sp", lambda s_=s_: nc.sync.dma_start(out=cT[:, :, s_], in_=c3[s_].rearrange("(k p) -> p k", p=128)), w=["cT"])
        P.dma("sp", lambda: nc.sync.dma_start(out=bT[:], in_=b_ada.rearrange("(n p) -> p n", p=128)), w=["bT"])
        P.dma("sp", lambda: nc.sync.dma_start(out=lnT[:, 0:8], in_=ln1_g.rearrange("(k p) -> p k", p=128)), w=["lnT"])
        P.dma("sp", lambda: nc.sync.dma_start(out=lnT[:, 8:16], in_=ln2_g.rearrange("(k p) -> p k", p=128)), w=["lnT"])
        P.op("dve", lambda: nc.vector.memset(silT[:], 0.0), w=["silT"])
        P.op("act", lambda: nc.scalar.activation(out=silT[:, :, 0:NQ], in_=cT[:], func=AF.Silu), r=["cT", "silT"], w=["silT"])
        for s in range(NQ):
            P.op("dve", lambda s=s: nc.vector.tensor_copy(out=silrep[:, s, :, :], in_=silT[:, :, s:s + 1].to_broadcast([128, 8, 128])),
                 r=["silT"], w=["silrep"])
        role_of = {0: 0, 1: 1, 3: 2, 4: 3}
        for ci in range(24):
            wb_ = wch[ci % 4]
            wk = "wch%d" % (ci % 4)
            blk6 = ci // 4
            P.dma("sp", lambda ci=ci, wb_=wb_: nc.sync.dma_start(
                out=wb_[:], in_=w_ada[:, ci * 256:(ci + 1) * 256].rearrange("(k p) n -> p k n", p=128)), w=[wk])
            if blk6 in role_of:
                role = role_of[blk6]
                b = bank()
                for n2 in range(2):
                    for k in range(8):
                        P.op("pe", lambda n2=n2, k=k, b=b, wb_=wb_: nc.tensor.matmul(
                            ps[:, b, n2 * 4:n2 * 4 + 4], lhsT=wb_[:, k, n2 * 128:(n2 + 1) * 128], rhs=silT[:, k, :],
                            start=(k == 0), stop=(k == 7)), r=[wk, "silT"], w=pk(b))
                for n2 in range(2):
                    nt_ = (ci % 4) * 2 + n2
                    P.op("dve", lambda n2=n2, nt_=nt_, b=b, role=role, ci=ci: nc.vector.tensor_scalar(
                        out=modT[:, role, nt_, :], in0=ps[:, b, n2 * 4:n2 * 4 + NQ], scalar1=bT[:, ci * 2 + n2:ci * 2 + n2 + 1],
                        scalar2=None, op0=ALU.add), r=pk(b) + ["bT"], w=["modT"])
            else:
                gi = 0 if blk6 == 2 else 1
                P.dma("sp", lambda ci=ci: nc.sync.dma_start(out=brow[:], in_=b_ada[ci * 256:(ci + 1) * 256].partition_broadcast(128)),
                      w=["brow"])
                for s in range(NQ):
                    b = bank()
                    for k in range(8):
                        P.op("pe", lambda s=s, k=k, b=b, wb_=wb_: nc.tensor.matmul(
                            ps[:, b, 0:256], lhsT=silrep[:, s, k, :], rhs=wb_[:, k, :],
                            start=(k == 0), stop=(k == 7)), r=[wk, "silrep"], w=pk(b))
                    P.op("dve", lambda s=s, b=b: nc.vector.tensor_tensor(
                        out=gbs[:, s, :], in0=ps[:, b, 0:256], in1=brow[:, :], op=ALU.add), r=pk(b) + ["brow"], w=["gbs"])
                    c0 = (ci % 4) * 256
                    P.dma("sp", lambda s=s, gi=gi, c0=c0: nc.sync.dma_start(out=gbd[gi * NQ + s:gi * NQ + s + 1, c0:c0 + 256], in_=gbs[0:1, s, :]),
                          r=["gbs"], w=["gbd"])
        for (Gt, sht, sc_role, sh_role, lo) in ((G1T, sh1T, 1, 0, 0), (G2T, sh2T, 3, 2, 8)):
            P.op("dve", lambda Gt=Gt, sc_role=sc_role: nc.vector.tensor_scalar(
                out=Gt[:], in0=modT[:, sc_role, :, :], scalar1=1.0, scalar2=None, op0=ALU.add), r=["modT"], w=["GT"])
            P.op("dve", lambda Gt=Gt, lo=lo: nc.vector.tensor_tensor(
                out=Gt[:], in0=Gt[:], in1=lnT[:, lo:lo + 8].unsqueeze(2).to_broadcast([128, 8, NQ]), op=ALU.mult),
                 r=["lnT", "GT"], w=["GT"])
            P.op("dve", lambda sht=sht, sh_role=sh_role: nc.vector.tensor_copy(out=sht[:], in_=modT[:, sh_role, :, :]),
                 r=["modT"], w=["GT"])
        set_es.close()
        P.barrier()
        P.stage("ada")
        set_es = contextlib.ExitStack()

        ident_f = SBs("ident_f", [128, 128])
        bmask = SBs("bmask", [128, 128])
        rowmask = SBs("rowmask", [128, 2])
        P.dma("sp", lambda: nc.sync.dma_start(out=ident_f[:], in_=ident_in[:, :]), w=["ident_f"])
        P.dma("sp", lambda: nc.sync.dma_start(out=bmask[:], in_=bmask_in[:, :]), w=["bmask"])
        ssmin = []
        def inkey():
            ssmin.append("ssmin_%d" % len(ssmin))
            return [ssmin[-1]]
        P.dma("sp", lambda: nc.sync.dma_start(out=rowmask[:], in_=rowmask_in[:, :]), w=inkey())
        are = SBs("are", [128, 16]); aim = SBs("aim", [128, 16]); dtt = SBs("dtt", [128, 16])
        th = SBs("th", [128, 16]); rr = SBs("rr", [128, 16])
        cs = SBs("cs", [128, 16]); sn = SBs("sn", [128, 16]); t0 = SBs("t0", [128, 16]); t1 = SBs("t1", [128, 16])
        t2 = SBs("t2", [128, 16])
        Lre = SBs("Lre", [128, D + 1, 16]); Lim = SBs("Lim", [128, D + 1, 16])
        cre = SBs("cre", [128, 16]); cim = SBs("cim", [128, 16])
        Bre = SBs("Bre", [128, 16, 32]); Bim = SBs("Bim", [128, 16, 32])
        Bbre = SBs("Bbre", [128, 16, 32]); Bbim = SBs("Bbim", [128, 16, 32])
        Cx = SBs("Cx", [128, 2, 4, 128])
        Ctre = SBs("Ctre", [128, 16, 32]); Ctim = SBs("Ctim", [128, 16, 32])
        Vre = SBs("Vre", [128, D + 1, 16, 32]); Vim = SBs("Vim", [128, D + 1, 16, 32])
        wt1 = SBs("wt1", [128, 16, 32]); wt2 = SBs("wt2", [128, 16, 32])
        Wst = SBs("Wst", [128, 2, 16, 32])
        Dcol = SBs("Dcol", [128, 4])
        halfpi = SBs("halfpi", [128, 1])
        ST = ["ssmsetup"]
        with nc.allow_non_contiguous_dma(reason="tiny setup loads"):
            for e in range(2):
                P.dma("sp", lambda e=e: nc.sync.dma_start(out=are[64 * e:64 * e + 64, :], in_=A_re.rearrange("(a e) p -> e p a", e=2)[e]), w=inkey())
                P.dma("sp", lambda e=e: nc.sync.dma_start(out=aim[64 * e:64 * e + 64, :], in_=A_im.rearrange("(a e) p -> e p a", e=2)[e]), w=inkey())
                P.dma("sp", lambda e=e: nc.sync.dma_start(
                    out=dtt[64 * e:64 * e + 64, :], in_=log_dt.rearrange("(a e) -> e a", e=2)[e].partition_broadcast(64)), w=inkey())
            P.dma("sp", lambda: nc.sync.dma_start(out=Dcol[:], in_=ssm_D.rearrange("(k p) -> p k", p=128)), w=inkey())
        P.op("pool", lambda: nc.gpsimd.memset(Bre[:], 0.0), w=["Bre_z"])
        P.op("pool", lambda: nc.gpsimd.memset(Bim[:], 0.0), w=["Bim_z"])
        P.op("pool", lambda: nc.gpsimd.memset(Cx[:], 0.0), w=["Cx_z"])
        P.op("pool", lambda: nc.gpsimd.memset(halfpi[:], math.pi / 2), w=inkey())
        for e in range(2):
            for (Bt, Bsrc) in ((Bre, B_re), (Bim, B_im)):
                P.dma("sp", lambda e=e, Bt=Bt, Bsrc=Bsrc: nc.sync.dma_start(
                    out=Bt[64 * e:64 * e + 64, :, 16 * e:16 * e + 16],
                    in_=Bsrc.rearrange("(a e) p h -> e p a h", e=2)[e]), r=["Bre_z", "Bim_z"], w=inkey())
            for ci_, Csrc in enumerate((C_re, C_im)):
                for ct in range(4):
                    for q in range(4):
                        P.dma("sp", lambda e=e, ci_=ci_, Csrc=Csrc, ct=ct, q=q: nc.sync.dma_start(
                            out=Cx[32 * q + 16 * e:32 * q + 16 * e + 16, ci_, ct, 64 * e:64 * e + 64],
                            in_=Csrc[8 * ct + 2 * q + e]), r=["Cx_z"], w=inkey())

        V = nc.vector
        P.op("dve", lambda: nc.vector.memset(t2[:], 0.0), r=list(ssmin) + ["Bre_z", "Bim_z", "Cx_z"], w=ST)

        def dv(fn):
            P.op("dve", fn, r=ST, w=ST)

        def cmul(ore, oim, are_, aim_, bre_, bim_, ta, tb):
            dv(lambda: V.tensor_tensor(out=ta, in0=are_, in1=bre_, op=ALU.mult))
            dv(lambda: V.tensor_tensor(out=tb, in0=aim_, in1=bim_, op=ALU.mult))
            dv(lambda: V.tensor_tensor(out=ore, in0=ta, in1=tb, op=ALU.subtract))
            dv(lambda: V.tensor_tensor(out=ta, in0=are_, in1=bim_, op=ALU.mult))
            dv(lambda: V.tensor_tensor(out=tb, in0=aim_, in1=bre_, op=ALU.mult))
            dv(lambda: V.tensor_tensor(out=oim, in0=ta, in1=tb, op=ALU.add))

        P.op("act", lambda: nc.scalar.activation(out=dtt[:], in_=dtt[:], func=AF.Exp), r=ST, w=ST)
        dv(lambda: V.tensor_tensor(out=th[:], in0=aim[:], in1=dtt[:], op=ALU.mult))
        dv(lambda: V.tensor_tensor(out=rr[:], in0=are[:], in1=dtt[:], op=ALU.mult))
        P.op("act", lambda: nc.scalar.activation(out=sn[:], in_=th[:], func=AF.Sin, scale=1.0 / 16), r=ST, w=ST)
        P.op("act", lambda: nc.scalar.activation(out=cs[:], in_=th[:], func=AF.Sin, scale=1.0 / 16, bias=halfpi[:, 0:1]), r=ST, w=ST)
        for _ in range(4):
            dv(lambda: V.tensor_tensor(out=t0[:], in0=cs[:], in1=cs[:], op=ALU.mult))
            dv(lambda: V.tensor_tensor(out=t1[:], in0=sn[:], in1=sn[:], op=ALU.mult))
            dv(lambda: V.tensor_tensor(out=t2[:], in0=cs[:], in1=sn[:], op=ALU.mult))
            dv(lambda: V.tensor_tensor(out=cs[:], in0=t0[:], in1=t1[:], op=ALU.subtract))
            dv(lambda: V.tensor_scalar(out=sn[:], in0=t2[:], scalar1=2.0, scalar2=None, op0=ALU.mult))
        P.op("act", lambda: nc.scalar.activation(out=t0[:], in_=rr[:], func=AF.Exp), r=ST, w=ST)
        P.op("act", lambda: nc.scalar.activation(out=rDt[:], in_=rr[:], func=AF.Exp, scale=float(D)), r=ST, w=ST)
        dv(lambda: V.memset(Lre[:, 0, :], 1.0))
        dv(lambda: V.memset(Lim[:, 0, :], 0.0))
        dv(lambda: V.tensor_tensor(out=Lre[:, 1, :], in0=t0[:], in1=cs[:], op=ALU.mult))
        dv(lambda: V.tensor_tensor(out=Lim[:, 1, :], in0=t0[:], in1=sn[:], op=ALU.mult))
        for k in range(2, D + 1):
            cmul(Lre[:, k, :], Lim[:, k, :], Lre[:, k - 1, :], Lim[:, k - 1, :], Lre[:, 1, :], Lim[:, 1, :], t1[:], t2[:])
        dv(lambda: V.reciprocal(out=t0[:], in_=rDt[:]))
        dv(lambda: V.tensor_tensor(out=E1re[:], in0=Lre[:, D, :], in1=t0[:], op=ALU.mult))
        dv(lambda: V.tensor_tensor(out=E1im[:], in0=Lim[:, D, :], in1=t0[:], op=ALU.mult))
        dv(lambda: V.tensor_tensor(out=t0[:], in0=are[:], in1=are[:], op=ALU.mult))
        dv(lambda: V.tensor_tensor(out=t1[:], in0=aim[:], in1=aim[:], op=ALU.mult))
        dv(lambda: V.tensor_tensor(out=t0[:], in0=t0[:], in1=t1[:], op=ALU.add))
        dv(lambda: V.reciprocal(out=t0[:], in_=t0[:]))
        dv(lambda: V.tensor_scalar(out=cs[:], in0=Lre[:, 1, :], scalar1=-1.0, scalar2=None, op0=ALU.add))
        dv(lambda: V.tensor_tensor(out=t1[:], in0=cs[:], in1=are[:], op=ALU.mult))
        dv(lambda: V.tensor_tensor(out=t2[:], in0=Lim[:, 1, :], in1=aim[:], op=ALU.mult))
        dv(lambda: V.tensor_tensor(out=t1[:], in0=t1[:], in1=t2[:], op=ALU.add))
        dv(lambda: V.tensor_tensor(out=cre[:], in0=t1[:], in1=t0[:], op=ALU.mult))
        dv(lambda: V.tensor_tensor(out=t1[:], in0=Lim[:, 1, :], in1=are[:], op=ALU.mult))
        dv(lambda: V.tensor_tensor(out=t2[:], in0=cs[:], in1=aim[:], op=ALU.mult))
        dv(lambda: V.tensor_tensor(out=t1[:], in0=t1[:], in1=t2[:], op=ALU.subtract))
        dv(lambda: V.tensor_tensor(out=cim[:], in0=t1[:], in1=t0[:], op=ALU.mult))
        bc = lambda col: col.unsqueeze(2).to_broadcast([128, 16, 32])
        cmul(Bbre[:], Bbim[:], Bre[:], Bim[:], bc(cre[:]), bc(cim[:]), wt1[:], wt2[:])
        for ci_, Ct in enumerate((Ctre, Ctim)):
            for ct in range(4):
                b = bank()
                P.op("pe", lambda ci_=ci_, ct=ct, b=b: nc.tensor.transpose(ps[:, b, 0:128], Cx[:, ci_, ct, :], ident_f[:]),
                     r=ST + ["ident_f"], w=pk(b))
                P.op("dve", lambda Ct=Ct, ct=ct, b=b: V.tensor_copy(
                    out=Ct[:, 4 * ct:4 * ct + 4, :], in_=ps[:, b, 0:128].rearrange("p (q c) -> p q c", q=4)), r=pk(b) + ST, w=ST)
        for k in range(D + 1):
            lr = bc(Lre[:, k, :]); li = bc(Lim[:, k, :])
            dv(lambda lr=lr: V.tensor_tensor(out=wt1[:], in0=Ctre[:], in1=lr, op=ALU.mult))
            dv(lambda li=li: V.tensor_tensor(out=wt2[:], in0=Ctim[:], in1=li, op=ALU.mult))
            dv(lambda k=k: V.tensor_tensor(out=Vre[:, k, :, :], in0=wt1[:], in1=wt2[:], op=ALU.subtract))
            dv(lambda li=li: V.tensor_tensor(out=wt1[:], in0=Ctre[:], in1=li, op=ALU.mult))
            dv(lambda lr=lr: V.tensor_tensor(out=wt2[:], in0=Ctim[:], in1=lr, op=ALU.mult))
            dv(lambda: V.tensor_tensor(out=wt1[:], in0=wt1[:], in1=wt2[:], op=ALU.add))
            dv(lambda k=k: V.tensor_scalar(out=Vim[:, k, :, :], in0=wt1[:], scalar1=-1.0, scalar2=None, op0=ALU.mult))
        P.op("pool", lambda: nc.gpsimd.memset(Vtab[:], 0.0), w=ST)
        for j in range(D):
            for c_, Vs in enumerate((Vre, Vim)):
                for par in range(2):
                    dv(lambda j=j, c_=c_, Vs=Vs, par=par: V.tensor_copy(
                        out=Vtab[:, par::2, j, c_, 32 * par:32 * par + 32], in_=Vs[:, j + 1, par::2, :]))
        for j in range(D):
            k = D - 1 - j
            cmul(Wst[:, 0, :, :], Wst[:, 1, :, :], Bbre[:], Bbim[:], bc(Lre[:, k, :]), bc(Lim[:, k, :]), wt1[:], wt2[:])
            for c_ in range(2):
                for ct in range(4):
                    b = bank()
                    P.op("pe", lambda c_=c_, ct=ct, b=b: nc.tensor.transpose(
                        ps[:, b, 0:128], Wst[:, c_, 4 * ct:4 * ct + 4, :].rearrange("p q c -> p (q c)"), ident_f[:]),
                         r=ST + ["ident_f"], w=pk(b))
                    for h in range(2):
                        for q2 in range(2):
                            P.op("dve", lambda c_=c_, ct=ct, b=b, h=h, q2=q2, j=j: V.tensor_scalar(
                                out=Wtab[64 * h:64 * h + 64, ct, j, c_, q2, :], in0=ps[64 * h:64 * h + 64, b, 0:128],
                                scalar1=rowmask[64 * h:64 * h + 64, q2:q2 + 1], scalar2=None, op0=ALU.mult),
                                 r=pk(b) + ST, w=ST)
        for d_ in range(D):
            for ct in range(4):
                b = bank()
                for qq in range(4):
                    for c_, (Bb, Vs) in enumerate(((Bbre, Vre), (Bbim, Vim))):
                        a = 4 * ct + qq
                        P.op("pe", lambda b=b, Bb=Bb, Vs=Vs, a=a, qq=qq, d_=d_, c_=c_: nc.tensor.matmul(
                            ps[:, b, 32 * qq:32 * qq + 32], lhsT=Bb[:, 4 * (a // 4):4 * (a // 4) + 4, :].rearrange("p q c -> p (q c)"),
                            rhs=Vs[:, d_, a, :], start=(c_ == 0), stop=(c_ == 1)), r=ST, w=pk(b))
                P.op("dve", lambda b=b, ct=ct, d_=d_: V.tensor_tensor(out=Ktab[:, ct, d_, :], in0=ps[:, b, 0:128], in1=bmask[:], op=ALU.mult),
                     r=pk(b) + ["bmask"] + ST, w=ST)
        for ct in range(4):
            dv(lambda ct=ct: V.scalar_tensor_tensor(out=Ktab[:, ct, 0, :], in0=ident_f[:], scalar=Dcol[:, ct:ct + 1],
                                                   in1=Ktab[:, ct, 0, :], op0=ALU.mult, op1=ALU.add))
        dv(lambda: V.memset(Ere[:, :, 0:1], 1.0))
        dv(lambda: V.memset(Eim[:, :, 0:1], 0.0))
        dv(lambda: V.tensor_copy(out=Ere[:, :, 1], in_=E1re[:]))
        dv(lambda: V.tensor_copy(out=Eim[:, :, 1], in_=E1im[:]))
        n = 2
        while n < 64:
            cmul(Ere[:, :, n], Eim[:, :, n], Ere[:, :, n - 1], Eim[:, :, n - 1], Ere[:, :, 1], Eim[:, :, 1], t1[:], t2[:])
            bn = lambda col, n=n: col.unsqueeze(2).to_broadcast([128, 16, n - 1])
            cmul(Ere[:, :, n + 1:2 * n], Eim[:, :, n + 1:2 * n], Ere[:, :, 1:n], Eim[:, :, 1:n],
                 bn(Ere[:, :, n]), bn(Eim[:, :, n]), wt1[:, :, 0:n - 1], wt2[:, :, 0:n - 1])
            n *= 2

        set_es.close()
        P.barrier()
        P.stage("ssmtab")
        p1w_es = contextlib.ExitStack()
        SB = lambda name, shape, dt=F32: p1w_es.enter_context(nc.sbuf_tensor("sb_" + name, list(shape), dt))
        sq = SB("sq", [128, 640], BF16)
        ssqk = SB("ssqk", [128, 10]); rsqk = SB("rsqk", [128, 10])
        qn = SB("qn", [128, 640])
        ra = SB("ra", [128, 10, 32]); rb = SB("rb", [128, 10, 32])
        qr = SB("qr", [128, 512], BF16)
        kr = SB("kr", [128, 128])
        krb = SB("krb", [128, 128], BF16)
        vf = SB("vf", [128, 128])
        qT2 = [SB("qT%d" % i, [128, 4, 256], BF16) for i in range(2)]
        kT3 = [SB("kT%d" % i, [128, 2, 128], BF16) for i in range(3)]
        Vaug3 = [SB("Vaug%d" % i, [128, 2, 2, 65], BF16) for i in range(3)]
        PT = SB("PT", [128, 2, 512], BF16)
        den = SB("den", [128, 8])
        attn = SB("attn", [128, 512])
        attnb = SB("attnb", [128, 512], BF16)
        ssa = SB("ssa", [128, 1]); rsa = SB("rsa", [128, 1])
        uT2 = [SB("uT%d" % i, [128, 4, 256], BF16) for i in range(2)]
        sA = SB("sA", [128, 16, 64]); sB = SB("sB", [128, 16, 64]); sC = SB("sC", [128, 16, 64]); sD = SB("sD", [128, 16, 64])
        Hbf2 = [SB("Hbf%d" % i, [128, 2, 16, 65], BF16) for i in range(2)]
        gT = SB("gT", [128, 4, 256], BF16)
        sig = SB("sig", [128, 256])
        soT = SB("soT", [128, 4, 256], BF16)
        sqT = SB("sqT", [128, 4, 256], BF16)
        rsb = SB("rsb", [128, 256])
        mTb = SB("mTb", [128, 8, 256], BF16)
        hinit = SB("hinit", [128, 2, 16])
        hst = SB("hst", [128, 16, 2])

        for i in range(3):
            P.op("pool", lambda i=i: nc.gpsimd.memset(Vaug3[i][:], 1.0), w=["Vaug%d" % i])

        seqs = []
        for s in range(NSEQ):
            seqs.append(dict(idx=s, T=S, prompt=True, row0=s * S))
        seqs.append(dict(idx=NSEQ, T=TS, prompt=False, row0=NSEQ * S))

        def xsrc(sq_, t0_, n):
            if sq_["prompt"]:
                return xp[sq_["idx"], t0_:t0_ + n, :]
            return xs[t0_:t0_ + n, :]

        def tok_view(ap2d, nt):
            if nt >= 128:
                return ap2d.rearrange("(t p) d -> p t d", p=128)
            return ap2d.unsqueeze(1)

        def norm_transpose(sq_, T, nt, TT, GT_, shT_, Xk, xo=0, ho=0, hk=("hT0", "hT1")):
            Xk = list(Xk) if isinstance(Xk, (list, tuple)) else [Xk]
            hk = list(hk) if isinstance(hk, (list, tuple)) else [hk]
            s = sq_["idx"]
            for tt in range(TT):
                P.op("act", lambda tt=tt: nc.scalar.activation(out=junk[0:nt, :], in_=X[0:nt, xo + tt, :], func=AF.Square,
                                                               accum_out=ss4[0:nt, tt:tt + 1]), r=Xk, w=["junk", "ss4"])
            P.op("act", lambda: nc.scalar.activation(out=rs4[0:nt, 0:TT], in_=ss4[0:nt, 0:TT], func=AF.Sqrt, scale=1.0 / DM, bias=epsc[0:nt, :]),
                 r=["ss4", "epsc"], w=["rs4"])
            P.op("dve", lambda: nc.vector.reciprocal(out=rs4[0:nt, 0:TT], in_=rs4[0:nt, 0:TT]), r=["rs4"], w=["rs4"])
            for half in range((TT + 1) // 2):
                tts = list(range(2 * half, min(TT, 2 * half + 2)))
                for tt in tts:
                    P.op("act", lambda tt=tt: nc.scalar.activation(out=xn[0:nt, tt % 2, :], in_=X[0:nt, xo + tt, :], func=AF.Copy,
                                                                   scale=rs4[0:nt, tt:tt + 1]), r=Xk + ["rs4"], w=["xn"])
                w0 = 2 * half * 128
                wn = len(tts) * nt if nt < 128 else len(tts) * 128
                for k in range(8):
                    b = bank()
                    for tt in tts:
                        P.op("pe", lambda tt=tt, k=k, b=b: nc.tensor.transpose(
                            psb[b][:, (tt % 2) * 128:(tt % 2) * 128 + nt], xn[0:nt, tt % 2, k * 128:(k + 1) * 128], ident_b[0:nt, 0:nt]),
                             r=["xn", "ident_b"], w=pk(b))
                    P.op("dve", lambda k=k, b=b, w0=w0, wn=wn: nc.vector.tensor_scalar(
                        out=hT[:, k, ho + w0:ho + w0 + wn], in0=psb[b][:, 0:wn], scalar1=GT_[:, k, s:s + 1], scalar2=shT_[:, k, s:s + 1],
                        op0=ALU.mult, op1=ALU.add), r=pk(b) + ["GT"], w=hk)


        def mixer_front(sq_, blk, T, gb):
            P.stage("mix_%d_%d" % (sq_["idx"], blk))
            par = gb % 2
            xo = 2 * par
            ho = 256 * par
            hk = "hT%d" % par
            qT = qT2[par]; qk_ = "qT%d" % par
            uT = uT2[par]; uk_ = "uT%d" % par
            kTo = kT3[gb % 3]; Vo = Vaug3[gb % 3]; kk_ = "kT%d" % (gb % 3); vk_ = "Vaug%d" % (gb % 3)
            Hbf = Hbf2[par]; hbk = "Hbf%d" % par
            s = sq_["idx"]
            prompt = sq_["prompt"]
            t0_ = blk * T
            nt = min(T, 128)
            TT = max(1, T // 128)
            first = (blk == 0)
            last = (t0_ + T == sq_["T"])
            Nc = T // D
            Xk = "X%d" % par
            P.dma("sp", lambda: nc.sync.dma_start(out=X[0:nt, xo:xo + TT, :], in_=tok_view(xsrc(sq_, t0_, T), nt)), w=[Xk])
            norm_transpose(sq_, T, nt, TT, G1T, sh1T, Xk, xo=xo, ho=ho, hk=hk)
            P.stage("m_a")
            for tt in range(TT):
                bq = bank(); bkv = bank()
                for k in range(8):
                    P.op("pe", lambda k=k, tt=tt, bq=bq: nc.tensor.matmul(
                        ps[0:nt, bq, :], lhsT=hT[:, k, ho + tt * 128:ho + tt * 128 + nt], rhs=Win[:, k, 0:512], start=(k == 0), stop=(k == 7)),
                         r=[hk, "Win"], w=pk(bq))
                    P.op("pe", lambda k=k, tt=tt, bkv=bkv: nc.tensor.matmul(
                        ps[0:nt, bkv, 0:256], lhsT=hT[:, k, ho + tt * 128:ho + tt * 128 + nt], rhs=Win[:, k, 512:768], start=(k == 0), stop=(k == 7)),
                         r=[hk, "Win"], w=pk(bkv))
                P.op("act", lambda bq=bq: nc.scalar.activation(out=sq[0:nt, 0:512], in_=ps[0:nt, bq, :], func=AF.Square), r=pk(bq), w=["sq"])
                P.op("act", lambda bkv=bkv: nc.scalar.activation(out=sq[0:nt, 512:640], in_=ps[0:nt, bkv, 0:128], func=AF.Square), r=pk(bkv), w=["sq"])
                P.op("dve", lambda: nc.vector.tensor_reduce(out=ssqk[0:nt, :], in_=sq[0:nt, :].rearrange("p (h d) -> p h d", d=HD),
                                                            axis=AX.X, op=ALU.add), r=["sq"], w=["ssqk"])
                P.op("act", lambda: nc.scalar.activation(out=rsqk[0:nt, :], in_=ssqk[0:nt, :], func=AF.Sqrt, scale=1.0 / HD, bias=epsc[0:nt, :]),
                     r=["ssqk", "epsc"], w=["rsqk"])
                P.op("dve", lambda: nc.vector.reciprocal(out=rsqk[0:nt, :], in_=rsqk[0:nt, :]), r=["rsqk"], w=["rsqk"])
                P.op("dve", lambda bq=bq: nc.vector.tensor_tensor(
                    out=qn[0:nt, 0:512].rearrange("p (h d) -> p h d", d=HD), in0=ps[0:nt, bq, :].rearrange("p (h d) -> p h d", d=HD),
                    in1=rsqk[0:nt, 0:8].unsqueeze(2).to_broadcast([nt, 8, HD]), op=ALU.mult), r=pk(bq) + ["rsqk"], w=["qn"])
                P.op("dve", lambda bkv=bkv: nc.vector.tensor_tensor(
                    out=qn[0:nt, 512:640].rearrange("p (h d) -> p h d", d=HD), in0=ps[0:nt, bkv, 0:128].rearrange("p (h d) -> p h d", d=HD),
                    in1=rsqk[0:nt, 8:10].unsqueeze(2).to_broadcast([nt, 2, HD]), op=ALU.mult), r=pk(bkv) + ["rsqk"], w=["qn"])
                P.op("pool", lambda: nc.gpsimd.tensor_tensor(out=qn[0:nt, :], in0=qn[0:nt, :], in1=gqk[0:nt, :], op=ALU.mult),
                     r=["qn", "gqk"], w=["qn"])
                if prompt:
                    tix = (t0_ // 128) + tt
                    cosb = cosp[0:nt, tix, :]; sinb = sinp[0:nt, tix, :]
                else:
                    cosb = coss[0:nt, :]; sinb = sins[0:nt, :]
                cb = cosb.unsqueeze(1).to_broadcast([nt, 10, 32]); sbb = sinb.unsqueeze(1).to_broadcast([nt, 10, 32])
                q3 = qn[0:nt, :].rearrange("p (h x f) -> p h x f", x=2, f=32)
                x1 = q3[:, :, 0, :]; x2 = q3[:, :, 1, :]
                qr3 = qr[0:nt, :].rearrange("p (h x f) -> p h x f", x=2, f=32)
                kr3 = kr[0:nt, :].rearrange("p (h x f) -> p h x f", x=2, f=32)
                P.op("pool", lambda x1=x1, cb=cb: nc.gpsimd.tensor_tensor(out=ra[0:nt], in0=x1, in1=cb, op=ALU.mult), r=["qn", "rope"], w=["ra"])
                P.op("pool", lambda x2=x2, sbb=sbb: nc.gpsimd.tensor_tensor(out=rb[0:nt], in0=x2, in1=sbb, op=ALU.mult), r=["qn", "rope"], w=["rb"])
                P.op("dve", lambda qr3=qr3: nc.vector.tensor_tensor(out=qr3[:, :, 0, :], in0=ra[0:nt, 0:8], in1=rb[0:nt, 0:8], op=ALU.subtract),
                     r=["ra", "rb"], w=["qr"])
                P.op("dve", lambda kr3=kr3: nc.vector.tensor_tensor(out=kr3[:, :, 0, :], in0=ra[0:nt, 8:10], in1=rb[0:nt, 8:10], op=ALU.subtract),
                     r=["ra", "rb"], w=["kr"])
                P.op("pool", lambda x2=x2, cb=cb: nc.gpsimd.tensor_tensor(out=ra[0:nt], in0=x2, in1=cb, op=ALU.mult), r=["qn", "rope"], w=["ra"])
                P.op("pool", lambda x1=x1, sbb=sbb: nc.gpsimd.tensor_tensor(out=rb[0:nt], in0=x1, in1=sbb, op=ALU.mult), r=["qn", "rope"], w=["rb"])
                P.op("dve", lambda qr3=qr3: nc.vector.tensor_tensor(out=qr3[:, :, 1, :], in0=ra[0:nt, 0:8], in1=rb[0:nt, 0:8], op=ALU.add),
                     r=["ra", "rb"], w=["qr"])
                P.op("dve", lambda kr3=kr3: nc.vector.tensor_tensor(out=kr3[:, :, 1, :], in0=ra[0:nt, 8:10], in1=rb[0:nt, 8:10], op=ALU.add),
                     r=["ra", "rb"], w=["kr"])
                P.op("act", lambda: nc.scalar.copy(out=krb[0:nt, :], in_=kr[0:nt, :]), r=["kr"], w=["krb"])
                P.op("act", lambda bkv=bkv, tt=tt: nc.scalar.copy(
                    out=Vo[0:nt, tt, :, 0:64], in_=ps[0:nt, bkv, 128:256].rearrange("p (g d) -> p g d", g=2)), r=pk(bkv), w=[vk_])
                need_out = (not prompt) or (last and tt == TT - 1)
                if need_out:
                    P.op("act", lambda bkv=bkv: nc.scalar.copy(out=vf[0:nt, :], in_=ps[0:nt, bkv, 128:256]), r=pk(bkv), w=["vf"])
                    if prompt:
                        P.dma("sp", lambda: nc.sync.dma_start(out=kp[s, :, :], in_=kr[:, :]), r=["kr"])
                        P.dma("sp", lambda: nc.sync.dma_start(out=vp[s, :, :], in_=vf[:, :]), r=["vf"])
                    else:
                        P.dma("sp", lambda: nc.sync.dma_start(out=ks[:, :], in_=kr[0:nt, :]), r=["kr"])
                        P.dma("sp", lambda: nc.sync.dma_start(out=vs[:, :], in_=vf[0:nt, :]), r=["vf"])
                b = bank()
                for j in range(4):
                    P.op("pe", lambda j=j, b=b: nc.tensor.transpose(psb[b][:, j * 128:j * 128 + nt], qr[0:nt, j * 128:(j + 1) * 128], ident_b[0:nt, 0:nt]),
                         r=["qr", "ident_b"], w=pk(b))
                P.op("act", lambda b=b, tt=tt: nc.scalar.copy(
                    out=qT[:, :, tt * 128:tt * 128 + nt], in_=psb[b][:, 0:512].rearrange("p (j t) -> p j t", j=4)[:, :, 0:nt]), r=pk(b), w=[qk_])
                b2 = bank()
                P.op("pe", lambda b2=b2: nc.tensor.transpose(psb[b2][:, 0:nt], krb[0:nt, :], ident_b[0:nt, 0:nt]), r=["krb", "ident_b"], w=pk(b2))
                P.op("dve", lambda b2=b2, tt=tt: nc.vector.tensor_copy(out=kTo[:, tt, 0:nt], in_=psb[b2][:, 0:nt]), r=pk(b2), w=[kk_])
            P.stage("m_b")
            for ct in range(4):
                b = bank()
                for k in range(8):
                    P.op("pe", lambda k=k, ct=ct, b=b: nc.tensor.matmul(
                        ps[:, b, 0:T], lhsT=Win[:, k, 768 + ct * 128:768 + (ct + 1) * 128], rhs=hT[:, k, ho:ho + T], start=(k == 0), stop=(k == 7)),
                         r=[hk, "Win"], w=pk(b))
                P.op("act", lambda ct=ct, b=b: nc.scalar.copy(out=uT[:, ct, 0:T], in_=ps[:, b, 0:T]), r=pk(b), w=[uk_])

            P.stage("m_d")
            if first:
                if prompt:
                    P.op("dve", lambda: nc.vector.memset(Hre[:], 0.0), w=["H"])
                    P.op("dve", lambda: nc.vector.memset(Him[:], 0.0), w=["H"])
                else:
                    with nc.allow_non_contiguous_dma(reason="state load"):
                        for e in range(2):
                            P.dma("sp", lambda e=e: nc.sync.dma_start(out=hst[64 * e:64 * e + 64], in_=st0.rearrange("(a e) p c -> e p a c", e=2)[e]), w=["hst"])
                    P.op("dve", lambda: nc.vector.tensor_copy(out=Hre[:], in_=hst[:, :, 0]), r=["hst"], w=["H"])
                    P.op("dve", lambda: nc.vector.tensor_copy(out=Him[:], in_=hst[:, :, 1]), r=["hst"], w=["H"])
            bSS = bank(4)
            for c_ in range(2):
                for a in range(16):
                    ct, qq = a // 4, a % 4
                    h, q2 = qq // 2, qq % 2
                    bb = bSS + 2 * c_ + h
                    off = (ct * 2 + q2) * 64
                    for j in range(D):
                        P.op("pe", lambda c_=c_, ct=ct, h=h, q2=q2, bb=bb, off=off, j=j: nc.tensor.matmul(
                            ps[:, bb, off:off + Nc], lhsT=Wtab[64 * h:64 * h + 64, ct, j, c_, q2, :], rhs=uT[64 * h:64 * h + 64, ct, j:T:D],
                            start=(j == 0), stop=(j == D - 1)), r=[uk_, "Wtab"], w=pk(bb))
            kS = pk(bSS, 4)
            er = Ere[:, :, 0:Nc]; ei = Eim[:, :, 0:Nc]
            a_ = sA[:, :, 0:Nc]; b_ = sB[:, :, 0:Nc]; c__ = sC[:, :, 0:Nc]; d_ = sD[:, :, 0:Nc]
            hv = lambda t, h: t[:, :, 0:Nc].rearrange("p (ct h q2) n -> p h ct q2 n", ct=4, h=2)[:, h]
            pv = lambda c_, h: ps[:, bSS + 2 * c_ + h, :].rearrange("p (ct q2 n) -> p ct q2 n", ct=4, q2=2)[:, :, :, 0:Nc]
            for h in range(2):
                P.op("dve", lambda h=h: V.tensor_tensor(out=hv(sA, h), in0=pv(0, h), in1=hv(Ere, h), op=ALU.mult), r=kS + ["Etab"], w=["sA"])
                P.op("dve", lambda h=h: V.tensor_tensor(out=hv(sB, h), in0=pv(1, h), in1=hv(Eim, h), op=ALU.mult), r=kS + ["Etab"], w=["sB"])
                P.op("dve", lambda h=h: V.tensor_tensor(out=hv(sC, h), in0=pv(1, h), in1=hv(Ere, h), op=ALU.mult), r=kS + ["Etab"], w=["sC"])
                P.op("dve", lambda h=h: V.tensor_tensor(out=hv(sD, h), in0=pv(0, h), in1=hv(Eim, h), op=ALU.mult), r=kS + ["Etab"], w=["sD"])
            P.op("pool", lambda: nc.gpsimd.tensor_tensor(out=a_, in0=a_, in1=b_, op=ALU.add), r=["sA", "sB"], w=["sA"])
            P.op("pool", lambda: nc.gpsimd.tensor_tensor(out=c__, in0=c__, in1=d_, op=ALU.subtract), r=["sC", "sD"], w=["sC"])
            P.op("dve", lambda: V.tensor_tensor(out=hinit[:, 0, :], in0=E1re[:], in1=Hre[:], op=ALU.mult), r=["H", "Etab"], w=["hinit"])
            P.op("dve", lambda: V.tensor_tensor(out=hinit[:, 1, :], in0=E1im[:], in1=Him[:], op=ALU.mult), r=["H", "Etab"], w=["hinit"])
            P.op("dve", lambda: V.tensor_tensor(out=hinit[:, 0, :], in0=hinit[:, 0, :], in1=hinit[:, 1, :], op=ALU.subtract), r=["hinit"], w=["hinit"])
            P.op("dve", lambda: V.tensor_tensor(out=hinit[:, 1, :], in0=E1re[:], in1=Him[:], op=ALU.mult), r=["H", "Etab"], w=["hinit"])
            P.op("dve", lambda: V.tensor_tensor(out=hst[:, :, 0], in0=E1im[:], in1=Hre[:], op=ALU.mult), r=["H", "Etab"], w=["hst"])
            P.op("dve", lambda: V.tensor_tensor(out=hinit[:, 1, :], in0=hinit[:, 1, :], in1=hst[:, :, 0], op=ALU.add), r=["hinit", "hst"], w=["hinit"])
            P.op("act", lambda: nc.scalar.copy(out=Hbf[:, 0, :, 0], in_=Hre[:]), r=["H"], w=[hbk])
            P.op("act", lambda: nc.scalar.copy(out=Hbf[:, 1, :, 0], in_=Him[:]), r=["H"], w=[hbk])
            for a in range(16):
                P.op("dve", lambda a=a: V.tensor_tensor_scan(out=sB[:, a, 0:Nc], data0=rDt[:, a:a + 1].to_broadcast([128, Nc]), data1=sA[:, a, 0:Nc],
                                                           initial=hinit[:, 0, a:a + 1], op0=ALU.mult, op1=ALU.add),
                     r=["sA", "hinit", "Etab"], w=["sB"])
                P.op("dve", lambda a=a: V.tensor_tensor_scan(out=sD[:, a, 0:Nc], data0=rDt[:, a:a + 1].to_broadcast([128, Nc]), data1=sC[:, a, 0:Nc],
                                                           initial=hinit[:, 1, a:a + 1], op0=ALU.mult, op1=ALU.add),
                     r=["sC", "hinit", "Etab"], w=["sD"])
            P.op("pool", lambda: nc.gpsimd.tensor_tensor(out=a_, in0=b_, in1=er, op=ALU.mult), r=["sB", "Etab"], w=["sA"])
            P.op("pool", lambda: nc.gpsimd.tensor_tensor(out=c__, in0=d_, in1=ei, op=ALU.mult), r=["sD", "Etab"], w=["sC"])
            P.op("dve", lambda: V.tensor_tensor(out=a_, in0=a_, in1=c__, op=ALU.subtract), r=["sA", "sC"], w=["sA"])
            P.op("pool", lambda: nc.gpsimd.tensor_tensor(out=c__, in0=d_, in1=er, op=ALU.mult), r=["sD", "Etab", "sA"], w=["sC"])
            P.op("pool", lambda: nc.gpsimd.tensor_tensor(out=d_, in0=b_, in1=ei, op=ALU.mult), r=["sB", "Etab"], w=["sD"])
            P.op("dve", lambda: V.tensor_tensor(out=c__, in0=c__, in1=d_, op=ALU.add), r=["sC", "sD"], w=["sC"])
            P.op("act", lambda: nc.scalar.copy(out=Hbf[:, 0, :, 1:1 + Nc], in_=a_), r=["sA"], w=[hbk])
            P.op("act", lambda: nc.scalar.copy(out=Hbf[:, 1, :, 1:1 + Nc], in_=c__), r=["sC"], w=[hbk])
            P.op("dve", lambda: V.tensor_copy(out=Hre[:], in_=sA[:, :, Nc - 1]), r=["sA"], w=["H"])
            P.op("dve", lambda: V.tensor_copy(out=Him[:], in_=sC[:, :, Nc - 1]), r=["sC"], w=["H"])
            if last:
                P.op("dve", lambda: V.tensor_copy(out=hst[:, :, 0], in_=Hre[:]), r=["H"], w=["hst"])
                P.op("dve", lambda: V.tensor_copy(out=hst[:, :, 1], in_=Him[:]), r=["H"], w=["hst"])
                dst = hp[s] if prompt else hs
                with nc.allow_non_contiguous_dma(reason="state store"):
                    for e in range(2):
                        P.dma("sp", lambda dst=dst, e=e: nc.sync.dma_start(out=dst.rearrange("(a e) p c -> e p a c", e=2)[e], in_=hst[64 * e:64 * e + 64]), r=["hst"])

        def mixer_back(sq_, blk, T, gb):
            par = gb % 2
            xo = 2 * par
            ho = 256 * par
            hk = "hT%d" % par
            Xk = "X%d" % par
            qT = qT2[par]; qk_ = "qT%d" % par
            uT = uT2[par]; uk_ = "uT%d" % par
            kTo = kT3[gb % 3]; Vo = Vaug3[gb % 3]; kk_ = "kT%d" % (gb % 3); vk_ = "Vaug%d" % (gb % 3)
            Hbf = Hbf2[par]; hbk = "Hbf%d" % par
            s = sq_["idx"]
            prompt = sq_["prompt"]
            t0_ = blk * T
            nt = min(T, 128)
            TT = max(1, T // 128)
            first = (blk == 0)
            last = (t0_ + T == sq_["T"])
            Nc = T // D
            hb = (gb - 1) % 3 if prompt else (gb + 1) % 3
            kTh = kT3[hb]; Vh = Vaug3[hb]; kkh = "kT%d" % hb; vkh = "Vaug%d" % hb
            if first:
                P.dma("sp", lambda: nc.sync.dma_start(out=Gb[:], in_=gbd[s:s + 1, :].partition_broadcast(128)), r=["gbd"], w=["Gb"])
                for k in range(8):
                    P.dma("pool", lambda k=k: nc.gpsimd.dma_start(out=Wout[:, k, :], in_=w_out[k * 128:(k + 1) * 128, :]), w=["Wout%d" % k])
                    P.op("dve", lambda k=k: nc.vector.tensor_tensor(out=Wout[:, k, :], in0=Wout[:, k, :], in1=Gb[:, :], op=ALU.mult),
                         r=["Gb", "Wout%d" % k], w=["Wout%d" % k])
            P.stage("m_c")
            if not prompt:
                P.dma("sp", lambda: nc.sync.dma_start(out=tmp[:, 0:128], in_=ck[:, :]), w=["tmp"])
                P.dma("sp", lambda: nc.sync.dma_start(out=tmp[:, 128:256], in_=cv[:, :]), w=["tmp"])
                P.op("dve", lambda: nc.vector.tensor_copy(out=krb[:, :], in_=tmp[:, 0:128]), r=["tmp"], w=["krb"])
                b2 = bank()
                P.op("pe", lambda b2=b2: nc.tensor.transpose(psb[b2][:, 0:128], krb[:, :], ident_b[:, :]), r=["krb", "ident_b"], w=pk(b2))
                P.op("dve", lambda b2=b2: nc.vector.tensor_copy(out=kTh[:, 1, :], in_=psb[b2][:, 0:128]), r=pk(b2), w=[kkh])
                P.op("dve", lambda: nc.vector.tensor_copy(out=Vh[:, 1, :, 0:64], in_=tmp[:, 128:256].rearrange("p (g d) -> p g d", g=2)),
                     r=["tmp"], w=[vkh])
            for tt in range(TT):
                has_prev = (not prompt) or (not first) or tt > 0
                use_mask = prompt
                for g in range(2):
                    bS = bank(2)
                    prev_t = (kTo, Vo, tt - 1, kk_, vk_) if tt > 0 else (kTh, Vh, 1, kkh, vkh)
                    kts = ([prev_t + (False,)] if has_prev else []) + [(kTo, Vo, tt, kk_, vk_, True)]
                    for slot_, (kTx, Vx, kt, kkx, vkx, own) in enumerate(kts):
                        nk = nt if own else 128
                        bb = bS + (1 if own else 0)
                        P.op("pe", lambda g=g, kt=kt, nk=nk, bb=bb, tt=tt, use_mask=use_mask, kTx=kTx: nc.tensor.matmul(
                            ps[0:nk, bb, 0:4 * nt].rearrange("p (j t) -> p j t", j=4),
                            lhsT=kTx[64 * g:64 * g + 64, kt, 0:nk], rhs=qT[64 * g:64 * g + 64, :, tt * 128:tt * 128 + nt],
                            start=True, stop=(not use_mask)), r=[kkx, qk_], w=pk(bb))
                        if use_mask:
                            mi = 1 if own else 0
                            P.op("pe", lambda mi=mi, bb=bb: nc.tensor.matmul(
                                ps[:, bb, :], lhsT=mrow[0:1, mi * 128:(mi + 1) * 128], rhs=mcol[0:1, mi * 128:(mi + 1) * 128].unsqueeze(1).to_broadcast([1, 4, 128]), start=False, stop=True),
                                 r=["mrow", "mcol"], w=pk(bb))
                        P.op("act", lambda nk=nk, bb=bb, own=own: nc.scalar.activation(
                            out=PT[0:nk, 1 if own else 0, 0:4 * nt], in_=ps[0:nk, bb, 0:4 * nt], func=AF.Exp), r=pk(bb), w=["PT"])
                    bO = bank()
                    for j in range(4):
                        for slot_, (kTx, Vx, kt, kkx, vkx, own) in enumerate(kts):
                            nk = nt if own else 128
                            P.op("pe", lambda j=j, kt=kt, nk=nk, own=own, g=g, bO=bO, slot_=slot_, kts=kts, Vx=Vx: nc.tensor.matmul(
                                ps[0:nt, bO, j * 65:(j + 1) * 65], lhsT=PT[0:nk, 1 if own else 0, j * nt:(j + 1) * nt], rhs=Vx[0:nk, kt, g, :],
                                start=(slot_ == 0), stop=(slot_ == len(kts) - 1)), r=["PT", vkx], w=pk(bO))
                    o3 = ps[0:nt, bO, 0:260].rearrange("p (j c) -> p j c", c=65)
                    P.op("dve", lambda o3=o3, g=g: nc.vector.tensor_tensor(out=den[0:nt, 4 * g:4 * g + 4], in0=o3[:, :, 64], in1=esink[0:nt, 4 * g:4 * g + 4],
                                                                         op=ALU.add), r=pk(bO) + ["esink"], w=["den"])
                    P.op("dve", lambda g=g: nc.vector.reciprocal(out=den[0:nt, 4 * g:4 * g + 4], in_=den[0:nt, 4 * g:4 * g + 4]), r=["den"], w=["den"])
                    P.op("dve", lambda o3=o3, g=g: nc.vector.tensor_tensor(
                        out=attn[0:nt, 256 * g:256 * g + 256].rearrange("p (j d) -> p j d", d=HD), in0=o3[:, :, 0:64],
                        in1=den[0:nt, 4 * g:4 * g + 4].unsqueeze(2).to_broadcast([nt, 4, HD]), op=ALU.mult), r=pk(bO) + ["den"], w=["attn"])
                P.op("act", lambda: nc.scalar.activation(out=junk[0:nt, 0:512], in_=attn[0:nt, :], func=AF.Square, accum_out=ssa[0:nt, :]),
                     r=["attn"], w=["junk", "ssa"])
                P.op("act", lambda: nc.scalar.activation(out=rsa[0:nt, :], in_=ssa[0:nt, :], func=AF.Sqrt, scale=1.0 / 512, bias=epsc[0:nt, :]),
                     r=["ssa", "epsc"], w=["rsa"])
                P.op("dve", lambda: nc.vector.reciprocal(out=rsa[0:nt, :], in_=rsa[0:nt, :]), r=["rsa"], w=["rsa"])
                P.op("dve", lambda: nc.vector.scalar_tensor_tensor(out=attnb[0:nt, :], in0=attn[0:nt, :], scalar=rsa[0:nt, 0:1], in1=gao[0:nt, :],
                                                                   op0=ALU.mult, op1=ALU.mult), r=["attn", "rsa", "gao"], w=["attnb"])
                b = bank()
                for k in range(4):
                    P.op("pe", lambda k=k, b=b: nc.tensor.transpose(psb[b][:, k * 128:k * 128 + nt], attnb[0:nt, k * 128:(k + 1) * 128], ident_b[0:nt, 0:nt]),
                         r=["attnb", "ident_b"], w=pk(b))
                P.op("act", lambda b=b, tt=tt: nc.scalar.copy(
                    out=mTb[:, 0:4, tt * 128:tt * 128 + nt], in_=psb[b][:, 0:512].rearrange("p (j t) -> p j t", j=4)[:, :, 0:nt]), r=pk(b), w=["mTb"])

            P.stage("m_e")
            for ct in range(4):
                b = bank()
                for j in range(D):
                    nmm = (j + 1) + 8
                    i_ = 0
                    for jp in range(j + 1):
                        P.op("pe", lambda ct=ct, j=j, jp=jp, b=b, i_=i_, nmm=nmm: nc.tensor.matmul(
                            ps[:, b, j:T:D], lhsT=Ktab[:, ct, j - jp, :], rhs=uT[:, ct, jp:T:D], start=(i_ == 0), stop=(i_ == nmm - 1)),
                             r=[uk_, "Ktab"], w=pk(b))
                        i_ += 1
                    for qq in range(4):
                        a = 4 * ct + qq
                        h = qq // 2
                        for c_ in range(2):
                            P.op("pe", lambda a=a, h=h, c_=c_, j=j, b=b, i_=i_, nmm=nmm, qq=qq: nc.tensor.matmul(
                                ps[64 * h:64 * h + 64, b, j:T:D], lhsT=Vtab[:, a, j, c_, :], rhs=Hbf[:, c_, a, 0:Nc],
                                start=(i_ == 0), stop=(qq in (1, 3) and c_ == 1), tile_position=(0, 64 * h)), r=[hbk, "Vtab"], w=pk(b))
                            i_ += 1
                P.op("act", lambda ct=ct, b=b: nc.scalar.activation(out=gT[:, ct, 0:T], in_=ps[:, b, 0:T], func=AF.Gelu_apprx_tanh), r=pk(b), w=["gT"])
            P.stage("m_f")
            for co in range(4):
                b = bank()
                for ct in range(4):
                    P.op("pe", lambda co=co, ct=ct, b=b: nc.tensor.matmul(
                        ps[:, b, 0:T], lhsT=Wglu[:, ct, co * 128:(co + 1) * 128], rhs=gT[:, ct, 0:T], start=(ct == 0), stop=(ct == 3)),
                         r=["gT", "Wglu"], w=pk(b))
                P.op("act", lambda co=co, b=b: nc.scalar.activation(out=sig[:, 0:T], in_=ps[:, b, 0:T], func=AF.Sigmoid, bias=glubT[:, co:co + 1]),
                     r=pk(b) + ["glubT"], w=["sig"])
                P.op("dve", lambda co=co: V.tensor_tensor(out=soT[:, co, 0:T], in0=gT[:, co, 0:T], in1=sig[:, 0:T], op=ALU.mult),
                     r=["gT", "sig"], w=["soT"])
                P.op("pool", lambda co=co: nc.gpsimd.tensor_tensor(out=sqT[:, co, 0:T], in0=soT[:, co, 0:T], in1=soT[:, co, 0:T], op=ALU.mult),
                     r=["soT"], w=["sqT"])
            b = bank()
            for ct in range(4):
                P.op("pe", lambda ct=ct, b=b: nc.tensor.matmul(ps[:, b, 0:T], lhsT=ones_b[:, :], rhs=sqT[:, ct, 0:T], start=(ct == 0), stop=(ct == 3)),
                     r=["sqT", "ones_b"], w=pk(b))
            P.op("act", lambda b=b: nc.scalar.activation(out=rsb[:, 0:T], in_=ps[:, b, 0:T], func=AF.Sqrt, scale=1.0 / SSMW, bias=epsc[:, :]),
                 r=pk(b) + ["epsc"], w=["rsb"])
            P.op("dve", lambda: V.reciprocal(out=rsb[:, 0:T], in_=rsb[:, 0:T]), r=["rsb"], w=["rsb"])
            for co in range(4):
                P.op("dve", lambda co=co: V.scalar_tensor_tensor(out=mTb[:, 4 + co, 0:T], in0=soT[:, co, 0:T], scalar=gsoT[:, co:co + 1], in1=rsb[:, 0:T],
                                                               op0=ALU.mult, op1=ALU.mult), r=["soT", "rsb", "gsoT"], w=["mTb"])
            P.stage("m_g")
            for tt in range(TT):
                bo = bank(2)
                Xr = Xr2[tt % 2]; xrk = "Xr%d" % (tt % 2)
                P.dma("sp", lambda tt=tt, Xr=Xr: nc.sync.dma_start(out=Xr[0:nt, :], in_=xsrc(sq_, t0_ + tt * 128, nt)), w=[xrk])
                for hf in range(2):
                    for k in range(8):
                        P.op("pe", lambda k=k, hf=hf, tt=tt, bo=bo: nc.tensor.matmul(
                            ps[0:nt, bo + hf, :], lhsT=mTb[:, k, tt * 128:tt * 128 + nt], rhs=Wout[:, k, hf * 512:(hf + 1) * 512],
                            start=(k == 0), stop=(k == 7)), r=["mTb", "Wout%d" % k], w=pk(bo + hf))
                P.op("dve", lambda bo=bo, Xr=Xr: V.tensor_tensor(out=Xr[0:nt, :], in0=ps[0:nt, bo:bo + 2, :].rearrange("p b n -> p (b n)"), in1=Xr[0:nt, :], op=ALU.add),
                     r=pk(bo, 2) + [xrk], w=[xrk])
                rr0 = sq_["row0"] + t0_ + tt * 128
                P.dma("sp", lambda rr0=rr0, Xr=Xr: nc.sync.dma_start(out=x1d[rr0:rr0 + nt, :], in_=Xr[0:nt, :]), r=[xrk], w=["x1d%d" % (rr0 // 512)])

        mT = hT
        for o in ():
            pass

        T1 = 256
        blocks = []
        for sq_ in seqs:
            if sq_["prompt"]:
                for blk in range(sq_["T"] // T1):
                    blocks.append((sq_, blk, T1))
            else:
                blocks.append((sq_, 0, sq_["T"]))
        mixer_front(*blocks[0], 0)
        for gb, blk_ in enumerate(blocks):
            if gb + 1 < len(blocks):
                mixer_front(*blocks[gb + 1], gb + 1)
            mixer_back(*blk_, gb)

        p1w_es.close()
        p1_es.close()
        P.barrier()
        P.stage("phase1")
        p2_es = contextlib.ExitStack()
        SB = lambda name, shape, dt=F32: p2_es.enter_context(nc.sbuf_tensor("sb_" + name, list(shape), dt))
        aT = SB("aT", [128, NFT, 512], BF16)
        sg = SB("sg", [128, 2, 512])

        Wg = SB("Wg", [128, 8, DFF], BF16)
        Wu = SB("Wu", [128, 8, DFF], BF16)
        Wd = SB("Wd", [128, NFT, DM], BF16)
        NQF = 4
        fq = [(q * NFT // NFT) for q in range(NQF)]
        qb = [0, 6, 12, 17, 22]
        def wkey(nm, f):
            for q in range(NQF):
                if qb[q] <= f < qb[q + 1]:
                    return "%s_q%d" % (nm, q)
        for q in range(NQF):
            c0, c1 = qb[q] * 128, qb[q + 1] * 128
            for k in range(8):
                P.dma("pool", lambda k=k, c0=c0, c1=c1: nc.gpsimd.dma_start(out=Wg[:, k, c0:c1], in_=w_gate[k * 128:(k + 1) * 128, c0:c1]),
                      w=["Wg_q%d" % q])
                P.dma("pool", lambda k=k, c0=c0, c1=c1: nc.gpsimd.dma_start(out=Wu[:, k, c0:c1], in_=w_up[k * 128:(k + 1) * 128, c0:c1]),
                      w=["Wu_q%d" % q])
        for f in range(NFT):
            P.dma("pool", lambda f=f: nc.gpsimd.dma_start(out=Wd[:, f, :], in_=w_down[f * 128:(f + 1) * 128, :]),
                  w=["Wd_%d" % f])

        def ffn_dims(sq_, blk, T):
            t0_ = blk * T
            return sq_["idx"], sq_["prompt"], t0_, min(T, 128), max(1, T // 128), sq_["row0"] + t0_

        def ffn_front(sq_, blk, T):
            s, prompt, t0_, nt, TT, r0 = ffn_dims(sq_, blk, T)
            Xk = ["X0", "X1"]
            xkeys = ["x1d%d" % i for i in range(r0 // 512, (r0 + T - 1) // 512 + 1)]
            P.dma("sp", lambda: nc.sync.dma_start(out=X[0:nt, 0:TT, :], in_=tok_view(x1d[r0:r0 + T, :], nt)), r=xkeys, w=Xk)
            norm_transpose(sq_, T, nt, TT, G2T, sh2T, Xk)

        def ffn_gateup(sq_, blk, T):
            s, prompt, t0_, nt, TT, r0 = ffn_dims(sq_, blk, T)
            if blk == 0:
                P.dma("sp", lambda: nc.sync.dma_start(out=Gb[:], in_=gbd[NQ + s:NQ + s + 1, :].partition_broadcast(128)), r=["gbd"], w=["Gb"])
            for f in range(NFT):
                bg = bank(); bu = bank()
                for k in range(8):
                    P.op("pe", lambda f=f, k=k, bg=bg: nc.tensor.matmul(ps[:, bg, 0:T], lhsT=Wg[:, k, f * 128:(f + 1) * 128], rhs=hT[:, k, 0:T],
                                                                      start=(k == 0), stop=(k == 7)), r=["hT0", "hT1", wkey("Wg", f)], w=pk(bg), n=T)
                for k in range(8):
                    P.op("pe", lambda f=f, k=k, bu=bu: nc.tensor.matmul(ps[:, bu, 0:T], lhsT=Wu[:, k, f * 128:(f + 1) * 128], rhs=hT[:, k, 0:T],
                                                                      start=(k == 0), stop=(k == 7)), r=["hT0", "hT1", wkey("Wu", f)], w=pk(bu), n=T)
                P.op("act", lambda bg=bg, f=f: nc.scalar.activation(out=sg[:, f % 2, 0:T], in_=ps[:, bg, 0:T], func=AF.Silu), r=pk(bg), w=["sg%d" % (f % 2)], n=T)
                P.op("dve", lambda f=f, bu=bu: V.tensor_tensor(out=aT[:, f, 0:T], in0=ps[:, bu, 0:T], in1=sg[:, f % 2, 0:T], op=ALU.mult),
                     r=pk(bu) + ["sg%d" % (f % 2)], w=["aT%d" % f], n=T)

        def ffn_down(sq_, blk, T):
            s, prompt, t0_, nt, TT, r0 = ffn_dims(sq_, blk, T)
            for tt in range(TT):
                bo = bank(2)
                Xr = Xr2[tt % 2]; xrk = "Xr%d" % (tt % 2)
                rr0 = r0 + tt * 128
                P.dma("sp", lambda rr0=rr0, Xr=Xr: nc.sync.dma_start(out=Xr[0:nt, :], in_=x1d[rr0:rr0 + nt, :]), r=["x1d%d" % (rr0 // 512)], w=[xrk])
                for hf in range(2):
                    for f in range(NFT):
                        P.op("pe", lambda f=f, hf=hf, tt=tt, bo=bo: nc.tensor.matmul(
                            ps[0:nt, bo + hf, :], lhsT=aT[:, f, tt * 128:tt * 128 + nt], rhs=Wd[:, f, hf * 512:(hf + 1) * 512],
                            start=(f == 0), stop=(f == NFT - 1)), r=["aT%d" % f, "Wd_%d" % f], w=pk(bo + hf), n=512)
                P.op("dve", lambda bo=bo: V.tensor_tensor(out=tmp[0:nt, :], in0=ps[0:nt, bo:bo + 2, :].rearrange("p b n -> p (b n)"), in1=Gb[0:nt, :], op=ALU.mult),
                     r=pk(bo, 2) + ["Gb"], w=["tmp"], n=1024)
                P.op("pool", lambda Xr=Xr: nc.gpsimd.tensor_tensor(out=Xr[0:nt, :], in0=Xr[0:nt, :], in1=tmp[0:nt, :], op=ALU.add),
                     r=["tmp", xrk], w=[xrk], n=1024)
                if prompt:
                    dst = yp[s, t0_ + tt * 128:t0_ + tt * 128 + nt, :]
                else:
                    dst = ys[t0_ + tt * 128:t0_ + tt * 128 + nt, :]
                P.dma("sp", lambda dst=dst, Xr=Xr: nc.sync.dma_start(out=dst, in_=Xr[0:nt, :]), r=[xrk])

        T2 = 512
        fblocks = []
        for sq_ in seqs:
            if sq_["prompt"]:
                for blk in range(sq_["T"] // T2):
                    fblocks.append((sq_, blk, T2))
            else:
                fblocks.append((sq_, 0, sq_["T"]))
        ffn_front(*fblocks[0])
        for i_, fb in enumerate(fblocks):
            ffn_gateup(*fb)
            if i_ + 1 < len(fblocks):
                ffn_front(*fblocks[i_ + 1])
            ffn_down(*fb)

        import os
        if os.environ.get("KSCHED", "1") == "1":
            P.schedule(K=int(os.environ.get("KSCHEDK", "3")))
        with nc.allow_non_contiguous_dma(reason="small strided setup/state transfers"):
            P.emit()
        build_program.stats = P.stats
        p2_es.close()
    return nc


def _consts(S, TS):
    half = HD // 2
    inv = (10000.0 ** (-np.arange(half, dtype=np.float32) * 2.0 / HD)).astype(np.float32)
    pos_p = np.arange(S, dtype=np.float32)[:, None] * inv[None, :]
    pos_s = (PAST + np.arange(TS, dtype=np.float32))[:, None] * inv[None, :]
    bm = np.kron(np.eye(4, dtype=np.float32), np.ones((32, 32), np.float32))
    mrow = np.zeros((2, 128), np.float32)
    mrow[0, 0:64] = 1.0
    mrow[1, 64:128] = 1.0
    mcol = np.zeros((2, 128), np.float32)
    mcol[0, 64:128] = -30000.0
    mcol[1, 0:64] = -30000.0
    rowmask = np.zeros((128, 2), np.float32)
    for p in range(128):
        rowmask[p, (p // 32) % 2] = 1.0
    return dict(ident=np.eye(128, dtype=np.float32), bmask=bm, rowmask=rowmask,
                cosp=np.cos(pos_p).astype(np.float32), sinp=np.sin(pos_p).astype(np.float32),
                coss=np.cos(pos_s).astype(np.float32), sins=np.sin(pos_s).astype(np.float32),
                mrow=mrow.reshape(1, 256), mcol=mcol.reshape(1, 256))


_W_NAMES = ["w_ada", "b_ada", "ln1_g", "w_in", "q_norm_g", "k_norm_g", "attn_sinks", "ssm_A_re", "ssm_A_im", "ssm_log_dt",
            "ssm_B_re", "ssm_B_im", "ssm_C_re", "ssm_C_im", "ssm_D", "ssm_glu_w", "ssm_glu_b", "attn_out_g", "ssm_out_g",
            "w_out", "ln2_g", "w_gate", "w_up", "w_down"]


def kernel(**inputs):
    x_prompt = np.asarray(inputs["x_prompt"], np.float32)
    x_sample = np.asarray(inputs["x_sample"], np.float32)
    B, S, _ = x_prompt.shape
    DB, TS, _ = x_sample.shape
    ncores = DB
    NSEQ = B // ncores
    nc = build_program(S, NSEQ, TS)
    cst = _consts(S, TS)
    wmap = {n: np.ascontiguousarray(np.asarray(inputs[n], np.float32)[0]) for n in _W_NAMES}
    in_maps = []
    for c in range(ncores):
        m = dict(wmap)
        m.update(cst)
        m["xp"] = np.ascontiguousarray(x_prompt[c * NSEQ:(c + 1) * NSEQ])
        m["xs"] = np.ascontiguousarray(x_sample[c])
        m["ck"] = np.ascontiguousarray(np.asarray(inputs["cache_k"], np.float32)[0, c].reshape(128, 128))
        m["cv"] = np.ascontiguousarray(np.asarray(inputs["cache_v"], np.float32)[0, c].reshape(128, 128))
        m["st0"] = np.ascontiguousarray(np.asarray(inputs["state_ssm"], np.float32)[0, c])
        m["c3"] = np.ascontiguousarray(np.concatenate(
            [np.asarray(inputs["c_prompt"], np.float32)[c * NSEQ:(c + 1) * NSEQ], np.asarray(inputs["c_sample"], np.float32)[c:c + 1]], axis=0))
        in_maps.append(m)
    res = run_bass_kernel_spmd(nc, in_maps, core_ids=list(range(ncores)))
    R = res.results
    y_prompt = np.concatenate([r["yp"] for r in R], axis=0)
    y_sample = np.stack([r["ys"] for r in R], axis=0)
    k_prompt = np.concatenate([r["kp"] for r in R], axis=0).reshape(1, B, 128, NKV, HD)
    v_prompt = np.concatenate([r["vp"] for r in R], axis=0).reshape(1, B, 128, NKV, HD)
    ssm_prompt = np.concatenate([r["hp"] for r in R], axis=0).reshape(1, B, NG, NP, 2)
    k_sample = np.stack([r["ks"] for r in R], axis=0).reshape(1, DB, TS, NKV, HD)
    v_sample = np.stack([r["vs"] for r in R], axis=0).reshape(1, DB, TS, NKV, HD)
    ssm_sample = np.stack([r["hs"] for r in R], axis=0).reshape(1, DB, NG, NP, 2)
    return (y_prompt.astype(np.float32), y_sample.astype(np.float32), k_prompt.astype(np.float32), v_prompt.astype(np.float32),
            ssm_prompt.astype(np.float32), k_sample.astype(np.float32), v_sample.astype(np.float32), ssm_sample.astype(np.float32))
```
